# Optimizing a Trainium2 kernel written in Bass

```python
import jax
import jax.numpy as jnp
from jax import lax
import numpy as np


D_MODEL = 2048
BATCH = 4
SEQ = 4096
DEPTH = 4

A_WIDTH = D_MODEL // 2
B_WIDTH = D_MODEL - A_WIDTH
A_GROUPS = 8
A_GROUP_DIM = A_WIDTH // A_GROUPS
CHUNK = 128
CONV_WIDTH = 3
ATTN_HEAD_DIM = 128
ATTN_HEADS = D_MODEL // ATTN_HEAD_DIM
DILATED_BRANCHES = ((128, 1), (512, 4), (2048, 16))
ATTN_BLOCK = 128
ROPE_THETA = 10000.0
FFN_DIM = 4 * D_MODEL
N_EVEN = (DEPTH + 1) // 2
N_ODD = DEPTH // 2
RMS_EPS = 1e-6
LN_EPS = 1e-5

kernel_name = 'hybrid_gmlp_shortconv_dilated_attn_trunk'


def rmsnorm(x, g):
    xf = x.astype(jnp.float32)
    y = xf * lax.rsqrt(jnp.mean(xf * xf, axis=-1, keepdims=True) + RMS_EPS)
    return (y * g.astype(jnp.float32)).astype(x.dtype)


def rope(t, positions):
    dh = t.shape[-1]
    half = dh // 2
    inv_freq = ROPE_THETA ** (-jnp.arange(half, dtype=jnp.float32) * 2.0 / dh)
    ang = positions.astype(jnp.float32)[:, None] * inv_freq[None, :]
    cos = jnp.cos(ang)[None, :, None, :]
    sin = jnp.sin(ang)[None, :, None, :]
    t1 = t[..., :half].astype(jnp.float32)
    t2 = t[..., half:].astype(jnp.float32)
    out = jnp.concatenate([t1 * cos - t2 * sin, t2 * cos + t1 * sin], axis=-1)
    return out.astype(t.dtype)


def gmlp_shortconv_mixer(h, w_in, w_s, b_s, conv_w, w_out):
    bsz, s, _ = h.shape
    proj = h @ w_in
    a_u, a_v, g_b, g_c, b_x = jnp.split(
        proj, [A_WIDTH, 2 * A_WIDTH, 2 * A_WIDTH + B_WIDTH, 2 * A_WIDTH + 2 * B_WIDTH], axis=-1)

    a_u = jax.nn.gelu(a_u)
    a_v = jax.nn.gelu(a_v)
    vf = a_v.astype(jnp.float32)
    mu = jnp.mean(vf, axis=-1, keepdims=True)
    var = jnp.mean(jnp.square(vf - mu), axis=-1, keepdims=True)
    vn = ((vf - mu) * lax.rsqrt(var + LN_EPS)).astype(h.dtype)
    n_chunks = s // CHUNK
    vn = vn.reshape(bsz, n_chunks, CHUNK, A_GROUPS, A_GROUP_DIM)
    causal = jnp.tril(jnp.ones((CHUNK, CHUNK), dtype=bool))
    w_causal = jnp.where(causal[None], w_s, 0)
    mixed = jnp.einsum('gts,bnsgc->bntgc', w_causal, vn) + b_s.T[None, None, :, :, None]
    a_out = a_u * mixed.reshape(bsz, s, A_WIDTH)

    z = g_c * b_x
    zp = jnp.pad(z, ((0, 0), (CONV_WIDTH - 1, 0), (0, 0)))
    y = conv_w[0] * zp[:, 0:s]
    for tap in range(1, CONV_WIDTH):
        y = y + conv_w[tap] * zp[:, tap:tap + s]
    b_out = g_b * y

    return jnp.concatenate([a_out, b_out], axis=-1) @ w_out


def dilated_branch(q, k, v, window, dilation):
    bsz, s, nh, dh = q.shape
    n_steps = window // dilation
    blk = ATTN_BLOCK
    L = s // dilation
    nb = -(-L // blk)
    Lp = nb * blk

    def to_blocks(t):
        t = t.reshape(bsz, L, dilation, nh, dh).transpose(0, 2, 3, 1, 4)
        t = jnp.pad(t, ((0, 0), (0, 0), (0, 0), (0, Lp - L), (0, 0)))
        return t.reshape(bsz, dilation, nh, nb, blk, dh)

    def with_prev_block(t):
        prev = jnp.pad(t, ((0, 0), (0, 0), (0, 0), (1, 0), (0, 0), (0, 0)))[:, :, :, :-1]
        return jnp.concatenate([prev, t], axis=4)

    qb = to_blocks(q)
    kk = with_prev_block(to_blocks(k))
    vv = with_prev_block(to_blocks(v))

    scores = jnp.einsum('brhnqc,brhnkc->brhnqk', qb, kk).astype(jnp.float32) * (dh ** -0.5)
    qi = jnp.arange(blk)[:, None]
    ki = jnp.arange(2 * blk)[None, :]
    step = qi + blk - ki
    band = (step >= 0) & (step <= n_steps)
    key_idx = jnp.arange(nb)[:, None, None] * blk - blk + ki[None]
    mask = band[None] & (key_idx >= 0)
    scores = jnp.where(mask, scores, -jnp.inf)
    m = jnp.max(scores, axis=-1, keepdims=True)
    p = jnp.exp(scores - m)
    denom = jnp.sum(p, axis=-1, keepdims=True)
    o = jnp.einsum('brhnqk,brhnkc->brhnqc', (p / denom).astype(v.dtype), vv)
    lse = (m + jnp.log(denom))[..., 0]

    o = o.reshape(bsz, dilation, nh, Lp, dh)[:, :, :, :L].transpose(0, 3, 1, 2, 4).reshape(bsz, s, nh, dh)
    lse = lse.reshape(bsz, dilation, nh, Lp)[..., :L].transpose(0, 3, 1, 2).reshape(bsz, s, nh)
    return o, lse


def dilated_attention_mixer(h, w_qkv, w_o, positions):
    bsz, s, _ = h.shape
    qkv = h @ w_qkv
    q, k, v = jnp.split(qkv, 3, axis=-1)
    q = rope(q.reshape(bsz, s, ATTN_HEADS, ATTN_HEAD_DIM), positions)
    k = rope(k.reshape(bsz, s, ATTN_HEADS, ATTN_HEAD_DIM), positions)
    v = v.reshape(bsz, s, ATTN_HEADS, ATTN_HEAD_DIM)
    outs = []
    lses = []
    for window, dilation in DILATED_BRANCHES:
        o_i, lse_i = dilated_branch(q, k, v, window, dilation)
        outs.append(o_i.astype(jnp.float32))
        lses.append(lse_i)
    alpha = jax.nn.softmax(jnp.stack(lses, axis=0), axis=0)
    o = jnp.einsum('ibsh,ibshc->bshc', alpha, jnp.stack(outs, axis=0))
    return o.astype(h.dtype).reshape(bsz, s, ATTN_HEADS * ATTN_HEAD_DIM) @ w_o


def squared_relu_mlp(h, w_up, w_down):
    return jnp.square(jax.nn.relu(h @ w_up)) @ w_down


def setup_inputs(seed: int = 0) -> dict:
    key = jax.random.key(seed)
    ks = jax.random.split(key, 14)
    f32 = jnp.float32
    d = D_MODEL
    x = jax.random.normal(ks[0], (BATCH, SEQ, d), f32)
    norm_mix_pre = 1.0 + 0.05 * jax.random.normal(ks[1], (DEPTH, d), f32)
    norm_mix_post = 1.0 + 0.05 * jax.random.normal(ks[2], (DEPTH, d), f32)
    norm_mlp_pre = 1.0 + 0.05 * jax.random.normal(ks[3], (DEPTH, d), f32)
    norm_mlp_post = 1.0 + 0.05 * jax.random.normal(ks[4], (DEPTH, d), f32)
    in_cols = 2 * A_WIDTH + 3 * B_WIDTH
    w_in_ab = jax.random.normal(ks[5], (N_EVEN, d, in_cols), f32) * d ** -0.5
    w_spatial = jax.random.normal(ks[6], (N_EVEN, A_GROUPS, CHUNK, CHUNK), f32) * (0.5 * CHUNK ** -0.5)
    b_spatial = 1.0 + 0.1 * jax.random.normal(ks[7], (N_EVEN, A_GROUPS, CHUNK), f32)
    conv_w = jax.random.normal(ks[8], (N_EVEN, CONV_WIDTH, B_WIDTH), f32) * CONV_WIDTH ** -0.5
    w_out_ab = jax.random.normal(ks[9], (N_EVEN, A_WIDTH + B_WIDTH, d), f32) * (A_WIDTH + B_WIDTH) ** -0.5
    attn_width = ATTN_HEADS * ATTN_HEAD_DIM
    w_qkv = jax.random.normal(ks[10], (N_ODD, d, 3 * attn_width), f32) * d ** -0.5
    w_o = jax.random.normal(ks[11], (N_ODD, attn_width, d), f32) * attn_width ** -0.5
    w_up = jax.random.normal(ks[12], (DEPTH, d, FFN_DIM), f32) * d ** -0.5
    w_down = jax.random.normal(ks[13], (DEPTH, FFN_DIM, d), f32) * FFN_DIM ** -0.5
    return {'x': x, 'norm_mix_pre': norm_mix_pre, 'norm_mix_post': norm_mix_post,
            'norm_mlp_pre': norm_mlp_pre, 'norm_mlp_post': norm_mlp_post,
            'w_in_ab': w_in_ab, 'w_spatial': w_spatial, 'b_spatial': b_spatial,
            'conv_w': conv_w, 'w_out_ab': w_out_ab, 'w_qkv': w_qkv, 'w_o': w_o,
            'w_up': w_up, 'w_down': w_down}


def reference(x, norm_mix_pre, norm_mix_post, norm_mlp_pre, norm_mlp_post,
              w_in_ab, w_spatial, b_spatial, conv_w, w_out_ab, w_qkv, w_o,
              w_up, w_down):
    positions = jnp.arange(x.shape[1], dtype=jnp.int32)
    h = x
    for layer in range(DEPTH):
        hn = rmsnorm(h, norm_mix_pre[layer])
        if layer % 2 == 0:
            e = layer // 2
            mix = gmlp_shortconv_mixer(hn, w_in_ab[e], w_spatial[e], b_spatial[e], conv_w[e], w_out_ab[e])
        else:
            o = layer // 2
            mix = dilated_attention_mixer(hn, w_qkv[o], w_o[o], positions)
        h = h + rmsnorm(mix, norm_mix_post[layer])
        f = squared_relu_mlp(rmsnorm(h, norm_mlp_pre[layer]), w_up[layer], w_down[layer])
        h = h + rmsnorm(f, norm_mlp_post[layer])
    return h
```

```python
import math
from contextlib import ExitStack

import numpy as np
import concourse.bass as bass
import concourse.mybir as mybir
from concourse.bass_utils import run_bass_kernel_spmd

F32 = mybir.dt.float32
BF16 = mybir.dt.bfloat16
AF = mybir.ActivationFunctionType
ALU = mybir.AluOpType

D = 2048
SEQ = 4096
NT = SEQ // 512
FFN = 8192
DEPTH = 4
ENGS = ("pe", "act", "dve", "pool", "sp")


class Buf:
    __slots__ = ("name", "w", "r")

    def __init__(self, name):
        self.name = name
        self.w = None
        self.r = []


class Op:
    __slots__ = ("eng", "fn", "waits", "token")

    def __init__(self, eng, fn):
        self.eng = eng
        self.fn = fn
        self.waits = []
        self.token = None


class Sched:
    def __init__(self):
        self.q = {e: [] for e in ENGS}
        self.cnt = {}
        self.waited = {e: {} for e in ENGS}
        self.dma_sems = []

    def dsem(self, name):
        self.dma_sems.append(name)
        self.cnt[name] = 0
        return name

    def op(self, eng, fn, reads=(), writes=(), dma_sem=None):
        deps = set()
        for b in reads:
            if b.w is not None:
                deps.add(b.w)
        for b in writes:
            if b.w is not None:
                deps.add(b.w)
            deps.update(b.r)
        o = Op(eng, fn)
        if dma_sem is not None:
            key = dma_sem
            self.cnt[key] = self.cnt.get(key, 0) + 16
        else:
            key = eng
            self.cnt[key] = self.cnt.get(key, 0) + 1
        o.token = (key, self.cnt[key])
        wd = self.waited[eng]
        for (k, v) in sorted(deps):
            if k == "pe" and eng == "pe":
                continue
            if wd.get(k, 0) >= v:
                continue
            wd[k] = v
            o.waits.append((k, v))
        for b in reads:
            b.r.append(o.token)
        for b in writes:
            b.w = o.token
            b.r = []
        self.q[eng].append(o)
        return o

    def barrier(self, engines=ENGS):
        snap = {k: v for k, v in self.cnt.items() if v > 0}
        for e in engines:
            o = Op(e, None)
            wd = self.waited[e]
            for k, v in sorted(snap.items()):
                if wd.get(k, 0) >= v:
                    continue
                wd[k] = v
                o.waits.append((k, v))
            self.q[e].append(o)

    def emit(self, nc):
        with ExitStack() as es:
            keys = list(ENGS) + list(self.dma_sems)
            sems = {k: es.enter_context(nc.semaphore("s_" + k)) for k in keys}
            block = es.enter_context(nc.Block())
            engmap = {"pe": block.tensor, "act": block.scalar, "dve": block.vector,
                      "pool": block.gpsimd, "sp": block.sync}

            def body_for(e):
                def body(eng):
                    for o in self.q[e]:
                        for (k, v) in o.waits:
                            eng.wait_ge(sems[k], v)
                        if o.fn is None:
                            continue
                        ins = o.fn(eng)
                        k, v = o.token
                        ins.then_inc(sems[k], 1 if k in ENGS else 16)
                return body

            for e in ENGS:
                engmap[e](body_for(e))


def build_program(layers=(0, 1, 2, 3), parts=("mix", "mlp")):
    nc = bass.Bass("TRN2", target_bir_lowering=False)
    S = Sched()

    def din(name, shape):
        return nc.dram_tensor(name, list(shape), F32, kind="ExternalInput").ap()

    x = din("x", (SEQ, D))
    n_mix_pre = din("norm_mix_pre", (DEPTH, D))
    n_mix_post = din("norm_mix_post", (DEPTH, D))
    n_mlp_pre = din("norm_mlp_pre", (DEPTH, D))
    n_mlp_post = din("norm_mlp_post", (DEPTH, D))
    w_in_ab = din("w_in_ab", (2, D, 5120))
    w_spatial = din("w_spatial", (2, 8, 128, 128))
    b_spatial = din("b_spatial", (2, 8, 128))
    conv_w = din("conv_w", (2, 3, 1024))
    w_out_ab = din("w_out_ab", (2, D, D))
    w_qkv = din("w_qkv", (2, D, 3 * D))
    w_o = din("w_o", (2, D, D))
    w_up = din("w_up", (DEPTH, D, FFN))
    w_down = din("w_down", (DEPTH, FFN, D))
    c_ident = din("c_ident", (128, 128))
    c_tril = din("c_tril", (128, 128))
    c_mask = din("c_mask", (128, 256))
    c_cos = din("c_cos", (128, 32, 64))
    c_sin = din("c_sin", (128, 32, 64))
    y = nc.dram_tensor("y", [SEQ, D], F32, kind="ExternalOutput").ap()

    def dscr(name, shape, dt=BF16):
        return nc.dram_tensor(name, list(shape), dt).ap()

    win_b = [dscr(f"win_b{e}", (D, 5120)) for e in range(2)]
    wout_b = [dscr(f"wout_b{e}", (D, D)) for e in range(2)]
    wqkv_b = [dscr(f"wqkv_b{e}", (D, 3 * D)) for e in range(2)]
    wo_b = [dscr(f"wo_b{e}", (D, D)) for e in range(2)]
    wup_b = [dscr(f"wup_b{l}", (D, FFN)) for l in range(DEPTH)]
    wdn_b = [dscr(f"wdn_b{l}", (FFN, D)) for l in range(DEPTH)]
    qT_d = dscr("qT_d", (16, 128, SEQ))
    kT_d = dscr("kT_d", (16, 128, SEQ))
    v_d = dscr("v_d", (SEQ, D))
    oT_d = dscr("oT_d", (16, 128, SEQ))

    es = ExitStack()
    with es:
        def sb(name, shape, dt):
            return es.enter_context(nc.sbuf_tensor(name, list(shape), dt))

        ht = sb("ht", (128, 4, D), F32)
        acc = sb("acc", (128, 4, D), F32)
        hnT = sb("hnT", (128, 16, 512), BF16)
        xT2 = sb("xT2", (128, 16, 512), BF16)
        ring = [sb(f"ring{i}", (128, 8192), BF16) for i in range(4)]
        gbA = sb("gbA", (128, D), F32)
        gbB = sb("gbB", (128, D), F32)
        _xsb = sb("xsb", (128, D), BF16)
        tmpA = _xsb
        xsb = [_xsb, _xsb]
        zxf = sb("zxf", (128, 2064), F32)
        zx = zxf[:].rearrange("p (c w) -> p c w", c=4)
        zh = sb("zh", (128, 8, 2), F32)
        st = sb("st", (128, 64), F32)
        identb = sb("identb", (128, 128), BF16)
        onesb = sb("onesb", (128, 128), BF16)
        maskb = sb("maskb", (128, 256), BF16)
        trilf = sb("trilf", (128, 128), F32)
        wsT = sb("wsT", (128, 8, 128), BF16)
        bsb = sb("bsb", (128, 8, 128), F32)
        cwt = sb("cwt", (128, 3, 8), F32)
        vq = sb("vq", (128, 4096), BF16)
        vn = vq[:].rearrange("p (s c) -> p s c", s=4)
        qrb = [vq[:, i * 2048:(i + 1) * 2048].rearrange("p (s c) -> p s c", s=4) for i in range(2)]
        stg = [zxf[:, i * 1024:(i + 1) * 1024].bitcast(BF16).rearrange("p (s c) -> p s c", s=4) for i in range(2)]
        relu_t = [zxf[:, i * 512:(i + 1) * 512] for i in range(2)]
        Et = [sb(f"Et{i}", (128, 256), BF16) for i in range(4)]

        psF = [es.enter_context(nc.psum_tensor(f"psF{i}", [128, 512], F32)) for i in range(6)]
        psT = [es.enter_context(nc.psum_tensor(f"psT{i}", [128, 1024], BF16)) for i in range(2)]

        B_ht = [Buf(f"ht{s}") for s in range(4)]
        B_acc = [[Buf(f"acc{s}_{c}") for c in range(4)] for s in range(4)]
        B_hnT = Buf("hnT")
        B_xT2 = [Buf(f"xT2_{i}") for i in range(4)]
        B_ring = [Buf(f"ring{i}") for i in range(4)]
        B_gbA, B_gbB = Buf("gbA"), Buf("gbB")
        _bx = Buf("xsb")
        B_xsb = [_bx, _bx]
        B_tmpA = _bx
        B_zx = [Buf(f"zx{i}") for i in range(4)]
        B_zh = Buf("zh")
        B_st = [Buf(f"st{i}") for i in range(16)]
        B_const = Buf("const")
        B_ws = Buf("ws")
        B_vn = [Buf(f"vn{s}") for s in range(4)]
        B_qrb = [[Buf(f"qrb{i}_{s}") for s in range(4)] for i in range(2)]
        B_stg = [Buf("stg0"), Buf("stg1")]
        B_relu = [Buf("relu0"), Buf("relu1")]
        B_E = [Buf(f"E{i}") for i in range(4)]
        B_psF = [Buf(f"psF{i}") for i in range(6)]
        B_psT = [Buf("psT0"), Buf("psT1")]
        B_y = Buf("y")
        B_ytile = [Buf(f"y{t}") for t in range(NT)]
        B_w = {}

        sem_ring = [S.dsem(f"dr{i}") for i in range(4)]
        sem_ht = S.dsem("dht")
        sem_st = S.dsem("dstore")
        sem_g = [S.dsem("dgA"), S.dsem("dgB")]
        sem_misc = [S.dsem(f"dm{i}") for i in range(8)]
        sem_cast = [S.dsem(f"dcast{i}") for i in range(DEPTH)]
        sem_stg = [S.dsem("dstg0"), S.dsem("dstg1")]
        sem_qk = [S.dsem(f"dqk{i}") for i in range(4)]
        sem_v = [S.dsem(f"dv{i}") for i in range(2)]
        sem_o = S.dsem("dot")
        sem_x2 = S.dsem("dx2")

        state = {"ps": 0, "pt": 0, "ring": 0, "st": 0, "relu": 0, "E": 0}

        def next_ps():
            i = state["ps"]
            state["ps"] = (i + 1) % 6
            return i

        def next_st():
            i = state["st"]
            state["st"] = (i + 1) % 16
            return i

        def load_consts():
            S.op("sp", lambda e: e.dma_start(out=acc[:, 0, 0:128], in_=c_ident), writes=[B_acc[0][0]], dma_sem=sem_misc[0])
            S.op("dve", lambda e: e.tensor_copy(out=identb[:], in_=acc[:, 0, 0:128]), reads=[B_acc[0][0]], writes=[B_const])
            S.op("sp", lambda e: e.dma_start(out=acc[:, 1, 0:256], in_=c_mask), writes=[B_acc[1][0]], dma_sem=sem_misc[1])
            S.op("dve", lambda e: e.tensor_copy(out=maskb[:], in_=acc[:, 1, 0:256]), reads=[B_acc[1][0]], writes=[B_const])
            S.op("dve", lambda e: e.memset(onesb[:], 1.0), writes=[B_const])
            S.op("sp", lambda e: e.dma_start(out=trilf[:], in_=c_tril), writes=[B_const], dma_sem=sem_misc[2])
            S.op("dve", lambda e: e.memset(zh[:], 0.0), writes=[B_zh])

        def cast_weights(l):
            allb = []

            def cast(dst, src, key, nsplit):
                rows = src.shape[0]
                step = rows // nsplit
                bufs = []
                for i in range(nsplit):
                    b = Buf(f"{key}_{i}")
                    S.op("pool", lambda e, i=i: e.dma_start(out=dst[i * step:(i + 1) * step, :], in_=src[i * step:(i + 1) * step, :]),
                         writes=[b], dma_sem=sem_cast[l])
                    bufs.append(b)
                    allb.append(b)
                B_w[key] = bufs
            if True:
                if l % 2 == 0:
                    cast(win_b[l // 2], w_in_ab[l // 2], f"win{l}", 2)
                    cast(wout_b[l // 2], w_out_ab[l // 2], f"wout{l}", 1)
                else:
                    cast(wqkv_b[l // 2], w_qkv[l // 2], f"wqkv{l}", 2)
                    cast(wo_b[l // 2], w_o[l // 2], f"wo{l}", 1)
                cast(wup_b[l], w_up[l], f"wup{l}", 2)
                cast(wdn_b[l], w_down[l], f"wdn{l}", 2)
            for b in allb:
                b.w = (sem_cast[l], S.cnt[sem_cast[l]])

        def ring_load(src_ap, view_fn, wbufs):
            i = state["ring"]
            state["ring"] = (i + 1) % 4
            S.op("sp", lambda e: e.dma_start(out=view_fn(ring[i]), in_=src_ap), reads=wbufs, writes=[B_ring[i]], dma_sem=sem_ring[i])
            return i

        def v_k512(t):
            return t[:].rearrange("p (k f) -> p k f", k=16)

        def v_f2048(t):
            return t[:].rearrange("p (k f) -> p k f", k=4)

        def load_gain(dst, bdst, sem, vec_ap):
            S.op("sp", lambda e: e.dma_start(out=dst[:], in_=vec_ap.partition_broadcast(128)), writes=[bdst], dma_sem=sem)

        def load_h(ti, src):
            S.op("sp", lambda e: e.dma_start(out=ht[:], in_=src[ti * 512:(ti + 1) * 512, :].rearrange("(s p) d -> p s d", p=128)),
                 reads=[B_ytile[ti]], writes=B_ht, dma_sem=sem_ht)

        def rstd_from_ss(ci, n, eps):
            bs = B_st[ci // 4]
            S.op("dve", lambda e: e.tensor_scalar(out=st[:, ci + 1:ci + 2], in0=st[:, ci:ci + 1], scalar1=1.0 / n, scalar2=eps, op0=ALU.mult, op1=ALU.add),
                 reads=[bs], writes=[bs])
            S.op("act", lambda e: e.activation(out=st[:, ci + 1:ci + 2], in_=st[:, ci + 1:ci + 2], func=AF.Sqrt), reads=[bs], writes=[bs])
            S.op("dve", lambda e: e.reciprocal(out=st[:, ci + 2:ci + 3], in_=st[:, ci + 1:ci + 2]), reads=[bs], writes=[bs])

        def norm_T():
            for s in range(4):
                si = next_st()
                ci = si * 4
                xb = s % 2
                S.op("act", lambda e, s=s, ci=ci: e.activation(out=tmpA[:], in_=ht[:, s, :], func=AF.Square, accum_out=st[:, ci:ci + 1]),
                     reads=[B_ht[s]], writes=[B_tmpA, B_st[si]])
                rstd_from_ss(ci, D, 1e-6)
                S.op("dve", lambda e, s=s, ci=ci, xb=xb: e.scalar_tensor_tensor(out=xsb[xb][:], in0=ht[:, s, :], scalar=st[:, ci + 2:ci + 3], in1=gbA[:], op0=ALU.mult, op1=ALU.mult),
                     reads=[B_ht[s], B_st[si], B_gbA], writes=[B_xsb[xb]])
                for half in range(2):
                    pt = state["pt"]
                    state["pt"] = 1 - pt
                    for j in range(8):
                        kc = half * 8 + j
                        S.op("pe", lambda e, pt=pt, j=j, kc=kc, xb=xb: e.transpose(out=psT[pt][:, j * 128:(j + 1) * 128], in_=xsb[xb][:, kc * 128:(kc + 1) * 128], identity=identb[:]),
                             reads=[B_xsb[xb], B_const], writes=[B_psT[pt]])
                    S.op("act", lambda e, pt=pt, half=half, s=s: e.activation(out=hnT[:, half * 8:(half + 1) * 8, s * 128:(s + 1) * 128],
                                                                          in_=psT[pt][:].rearrange("p (j t) -> p j t", j=8), func=AF.Copy),
                         reads=[B_psT[pt]], writes=[B_hnT])

        def finish_tile(ti):
            for s in range(4):
                si = next_st()
                ci = si * 4
                S.op("act", lambda e, s=s, ci=ci: e.activation(out=tmpA[:], in_=acc[:, s, :], func=AF.Square, accum_out=st[:, ci:ci + 1]),
                     reads=B_acc[s], writes=[B_tmpA, B_st[si]])
                rstd_from_ss(ci, D, 1e-6)
                S.op("dve", lambda e, s=s, ci=ci: e.scalar_tensor_tensor(out=acc[:, s, :], in0=acc[:, s, :], scalar=st[:, ci + 2:ci + 3], in1=gbB[:], op0=ALU.mult, op1=ALU.mult),
                     reads=B_acc[s] + [B_st[si], B_gbB], writes=B_acc[s])
                S.op("pool", lambda e, s=s: e.tensor_tensor(out=ht[:, s, :], in0=ht[:, s, :], in1=acc[:, s, :], op=ALU.add),
                     reads=B_acc[s] + [B_ht[s]], writes=[B_ht[s]])
            S.op("sp", lambda e: e.dma_start(out=y[ti * 512:(ti + 1) * 512, :].rearrange("(s p) d -> p s d", p=128), in_=ht[:]),
                 reads=B_ht, writes=[B_ytile[ti]], dma_sem=sem_st)

        def proj_tokmajor(w_b, wkey_bufs, n_units, first_group_done=False, unit_row0=0, xq_of_unit=None, first=True):
            for g in range(n_units // 2):
                slots = []
                for uu in range(2):
                    u = g * 2 + uu
                    r0 = unit_row0 + u * 512
                    slots.append(ring_load(w_b[r0:r0 + 512, :].rearrange("(k p) d -> p k d", p=128), v_f2048, wkey_bufs))
                for s in range(4):
                    for dc in range(4):
                        pi = next_ps()
                        for uu in range(2):
                            u = g * 2 + uu
                            xq = xq_of_unit(u)
                            for j in range(4):
                                S.op("pe", lambda e, pi=pi, xq=xq, j=j, s=s, dc=dc, sl=slots[uu], uu=uu: e.matmul(
                                    psF[pi][:], lhsT=xT2[:, xq * 4 + j, s * 128:(s + 1) * 128], rhs=v_f2048(ring[sl])[:, j, dc * 512:(dc + 1) * 512],
                                    start=(uu == 0 and j == 0), stop=(uu == 1 and j == 3)),
                                    reads=[B_xT2[xq], B_ring[slots[uu]]], writes=[B_psF[pi]])
                        if first and g == 0:
                            S.op("act", lambda e, pi=pi, s=s, dc=dc: e.activation(out=acc[:, s, dc * 512:(dc + 1) * 512], in_=psF[pi][:], func=AF.Copy),
                                 reads=[B_psF[pi]], writes=[B_acc[s][dc]])
                        else:
                            S.op("dve", lambda e, pi=pi, s=s, dc=dc: e.tensor_tensor(out=acc[:, s, dc * 512:(dc + 1) * 512], in0=psF[pi][:], in1=acc[:, s, dc * 512:(dc + 1) * 512], op=ALU.add),
                                 reads=[B_psF[pi], B_acc[s][dc]], writes=[B_acc[s][dc]])

        def mlp_layer(l, src):
            load_gain(gbA, B_gbA, sem_g[0], n_mlp_pre[l:l + 1, :])
            load_gain(gbB, B_gbB, sem_g[1], n_mlp_post[l:l + 1, :])
            for ti in range(NT):
                load_h(ti, src)
                norm_T()
                for g in range(8):
                    ub = g % 2
                    for uu in range(2):
                        c0 = (g * 2 + uu) * 512
                        sl = ring_load(wup_b[l][:, c0:c0 + 512].rearrange("(k p) f -> p k f", p=128), v_k512, B_w[f"wup{l}"])
                        for f in range(4):
                            pi = next_ps()
                            for kc in range(16):
                                S.op("pe", lambda e, pi=pi, sl=sl, kc=kc, f=f: e.matmul(psF[pi][:], lhsT=v_k512(ring[sl])[:, kc, f * 128:(f + 1) * 128], rhs=hnT[:, kc, :],
                                                                                     start=(kc == 0), stop=(kc == 15)),
                                     reads=[B_ring[sl], B_hnT], writes=[B_psF[pi]])
                            ri = state["relu"]
                            state["relu"] = 1 - ri
                            xq = ub * 2 + uu
                            S.op("act", lambda e, pi=pi, ri=ri: e.activation(out=relu_t[ri][:], in_=psF[pi][:], func=AF.Relu),
                                 reads=[B_psF[pi]], writes=[B_relu[ri]])
                            S.op("dve", lambda e, ri=ri, xq=xq, f=f: e.tensor_tensor(out=xT2[:, xq * 4 + f, :], in0=relu_t[ri][:], in1=relu_t[ri][:], op=ALU.mult),
                                 reads=[B_relu[ri]], writes=[B_xT2[xq]])
                    proj_tokmajor(wdn_b[l], B_w[f"wdn{l}"], 2, unit_row0=g * 1024, xq_of_unit=lambda u, ub=ub: ub * 2 + u, first=(g == 0))
                finish_tile(ti)

        def prep_even(e_idx):
            S.op("sp", lambda e: e.dma_start(out=acc[:, 0, 0:1024].rearrange("p (g s) -> p g s", g=8), in_=w_spatial[e_idx].rearrange("g t s -> t g s")),
                 writes=B_acc[0], dma_sem=sem_misc[3])
            S.op("dve", lambda e: e.tensor_tensor(out=xsb[0][:, 0:1024].rearrange("p (g s) -> p g s", g=8), in0=acc[:, 0, 0:1024].rearrange("p (g s) -> p g s", g=8),
                                                  in1=trilf[:].unsqueeze(1).broadcast_to([128, 8, 128]), op=ALU.mult),
                 reads=B_acc[0] + [B_const], writes=[B_xsb[0]])
            for g in range(8):
                S.op("pe", lambda e, g=g: e.transpose(out=psT[0][:, g * 128:(g + 1) * 128], in_=xsb[0][:, g * 128:(g + 1) * 128], identity=identb[:]),
                     reads=[B_xsb[0], B_const], writes=[B_psT[0]])
            S.op("act", lambda e: e.activation(out=wsT[:], in_=psT[0][:].rearrange("p (g t) -> p g t", g=8), func=AF.Copy), reads=[B_psT[0]], writes=[B_ws])
            S.op("sp", lambda e: e.dma_start(out=bsb[:].rearrange("p g t -> p (g t)"), in_=b_spatial[e_idx:e_idx + 1].rearrange("o g t -> o (g t)").partition_broadcast(128)),
                 writes=[B_ws], dma_sem=sem_misc[4])
            for k in range(3):
                S.op("sp", lambda e, k=k: e.dma_start(out=cwt[:, k, :], in_=conv_w[e_idx, k].rearrange("(j p) -> p j", p=128), allow_slow_non_contiguous=True),
                     writes=[B_ws], dma_sem=sem_misc[5])

        def even_mixer(l, src):
            e_idx = l // 2
            wb = win_b[e_idx]
            wk = B_w[f"win{l}"]
            prep_even(e_idx)
            load_gain(gbA, B_gbA, sem_g[0], n_mix_pre[l:l + 1, :])
            load_gain(gbB, B_gbB, sem_g[1], n_mix_post[l:l + 1, :])
            S.op("dve", lambda e: e.memset(zh[:], 0.0), reads=[], writes=[B_zh])

            def unit(u):
                return ring_load(wb[:, u * 512:(u + 1) * 512].rearrange("(k p) f -> p k f", p=128), v_k512, wk)

            def feat_mm(sl, c):
                pi = next_ps()
                for kc in range(16):
                    S.op("pe", lambda e, pi=pi, sl=sl, kc=kc, c=c: e.matmul(psF[pi][:], lhsT=v_k512(ring[sl])[:, kc, c * 128:(c + 1) * 128], rhs=hnT[:, kc, :],
                                                                         start=(kc == 0), stop=(kc == 15)),
                         reads=[B_ring[sl], B_hnT], writes=[B_psF[pi]])
                return pi

            for ti in range(NT):
                load_h(ti, src)
                norm_T()
                sl_av = [unit(2), unit(3)]
                for s in range(4):
                    for hf in range(2):
                        pi = next_ps()
                        for kc in range(16):
                            S.op("pe", lambda e, pi=pi, sl=sl_av[hf], kc=kc, s=s: e.matmul(psF[pi][:], lhsT=hnT[:, kc, s * 128:(s + 1) * 128], rhs=v_k512(ring[sl])[:, kc, :],
                                                                                 start=(kc == 0), stop=(kc == 15)),
                                 reads=[B_ring[sl_av[hf]], B_hnT], writes=[B_psF[pi]])
                        S.op("act", lambda e, pi=pi, hf=hf, s=s: e.activation(out=acc[:, 3, (s % 2) * 1024 + hf * 512:(s % 2) * 1024 + (hf + 1) * 512], in_=psF[pi][:], func=AF.Gelu_apprx_tanh),
                             reads=[B_psF[pi]], writes=[B_acc[3][(s % 2) * 2 + hf]])
                    avs = acc[:, 3, (s % 2) * 1024:(s % 2 + 1) * 1024]
                    bav = [B_acc[3][(s % 2) * 2], B_acc[3][(s % 2) * 2 + 1]]
                    si = next_st()
                    ci = si * 4
                    S.op("act", lambda e, avs=avs, ci=ci: e.activation(out=tmpA[:, 0:1024], in_=avs, func=AF.Copy, accum_out=st[:, ci:ci + 1]),
                         reads=bav, writes=[B_tmpA, B_st[si]])
                    S.op("act", lambda e, avs=avs, ci=ci: e.activation(out=tmpA[:, 1024:2048], in_=avs, func=AF.Square, accum_out=st[:, ci + 1:ci + 2]),
                         reads=bav, writes=[B_tmpA, B_st[si]])
                    S.op("dve", lambda e, ci=ci: e.tensor_scalar(out=st[:, ci:ci + 1], in0=st[:, ci:ci + 1], scalar1=1.0 / 1024, scalar2=0.0, op0=ALU.mult, op1=ALU.add),
                         reads=[B_st[si]], writes=[B_st[si]])
                    S.op("dve", lambda e, ci=ci: e.tensor_tensor(out=st[:, ci + 2:ci + 3], in0=st[:, ci:ci + 1], in1=st[:, ci:ci + 1], op=ALU.mult),
                         reads=[B_st[si]], writes=[B_st[si]])
                    S.op("dve", lambda e, ci=ci: e.scalar_tensor_tensor(out=st[:, ci + 1:ci + 2], in0=st[:, ci + 1:ci + 2], scalar=1.0 / 1024, in1=st[:, ci + 2:ci + 3], op0=ALU.mult, op1=ALU.subtract),
                         reads=[B_st[si]], writes=[B_st[si]])
                    S.op("dve", lambda e, ci=ci: e.tensor_scalar(out=st[:, ci + 1:ci + 2], in0=st[:, ci + 1:ci + 2], scalar1=1e-5, scalar2=0.0, op0=ALU.add, op1=ALU.add),
                         reads=[B_st[si]], writes=[B_st[si]])
                    S.op("act", lambda e, ci=ci: e.activation(out=st[:, ci + 1:ci + 2], in_=st[:, ci + 1:ci + 2], func=AF.Sqrt), reads=[B_st[si]], writes=[B_st[si]])
                    S.op("dve", lambda e, ci=ci: e.reciprocal(out=st[:, ci + 2:ci + 3], in_=st[:, ci + 1:ci + 2]), reads=[B_st[si]], writes=[B_st[si]])
                    S.op("dve", lambda e, avs=avs, ci=ci, s=s: e.tensor_scalar(out=vn[:, s, :], in0=avs, scalar1=st[:, ci:ci + 1], scalar2=st[:, ci + 2:ci + 3], op0=ALU.subtract, op1=ALU.mult),
                         reads=bav + [B_st[si]], writes=[B_vn[s]])
                for uu in range(2):
                    sl = unit(uu)
                    for c in range(4):
                        j = uu * 4 + c
                        pi = feat_mm(sl, c)
                        S.op("act", lambda e, pi=pi, c=c: e.activation(out=acc[:, 1, c * 512:(c + 1) * 512], in_=psF[pi][:], func=AF.Gelu_apprx_tanh),
                             reads=[B_psF[pi]], writes=[B_acc[1][c]])
                        pm = next_ps()
                        for s in range(4):
                            S.op("pe", lambda e, pm=pm, s=s, j=j: e.matmul(psF[pm][:, s * 128:(s + 1) * 128], lhsT=vn[:, s, j * 128:(j + 1) * 128], rhs=wsT[:, j, :], start=True, stop=True),
                                 reads=[B_vn[s], B_ws], writes=[B_psF[pm]])
                        S.op("dve", lambda e, pm=pm, c=c, j=j: e.tensor_tensor(out=acc[:, 2, c * 512:(c + 1) * 512].rearrange("p (s t) -> p s t", s=4), in0=psF[pm][:].rearrange("p (s t) -> p s t", s=4),
                                                                       in1=bsb[:, j, :].unsqueeze(1).broadcast_to([128, 4, 128]), op=ALU.add),
                             reads=[B_psF[pm], B_ws], writes=[B_acc[2][c]])
                        S.op("pool", lambda e, c=c, j=j: e.tensor_tensor(out=xT2[:, j, :], in0=acc[:, 2, c * 512:(c + 1) * 512], in1=acc[:, 1, c * 512:(c + 1) * 512], op=ALU.mult),
                             reads=[B_acc[2][c], B_acc[1][c]], writes=[B_xT2[j // 4]])
                for uu in range(2):
                    sl = unit(6 + uu)
                    for c in range(4):
                        pi = feat_mm(sl, c)
                        S.op("act", lambda e, pi=pi, c=c: e.activation(out=acc[:, 0, c * 512:(c + 1) * 512], in_=psF[pi][:], func=AF.Copy),
                             reads=[B_psF[pi]], writes=[B_acc[0][c]])
                    sl = unit(8 + uu)
                    for c in range(4):
                        j = uu * 4 + c
                        pi = feat_mm(sl, c)
                        S.op("dve", lambda e, pi=pi, c=c: e.tensor_tensor(out=zx[:, c, 2:514], in0=psF[pi][:], in1=acc[:, 0, c * 512:(c + 1) * 512], op=ALU.mult),
                             reads=[B_psF[pi], B_acc[0][c]], writes=[B_zx[c]])
                        S.op("pool", lambda e, c=c, j=j: e.tensor_copy(out=zx[:, c, 0:2], in_=zh[:, j, :]), reads=[B_zh], writes=[B_zx[c]])
                        S.op("pool", lambda e, c=c, j=j: e.tensor_copy(out=zh[:, j, :], in_=zx[:, c, 512:514]), reads=[B_zx[c]], writes=[B_zh])
                        yv = acc[:, 0, c * 512:(c + 1) * 512]
                        S.op("pool", lambda e, c=c, j=j, yv=yv: e.tensor_scalar(out=yv, in0=zx[:, c, 0:512], scalar1=cwt[:, 0, j:j + 1], scalar2=0.0, op0=ALU.mult, op1=ALU.add),
                             reads=[B_zx[c], B_ws], writes=[B_acc[0][c]])
                        S.op("dve", lambda e, c=c, j=j, yv=yv: e.scalar_tensor_tensor(out=yv, in0=zx[:, c, 1:513], scalar=cwt[:, 1, j:j + 1], in1=yv, op0=ALU.mult, op1=ALU.add),
                             reads=[B_zx[c], B_ws, B_acc[0][c]], writes=[B_acc[0][c]])
                        S.op("dve", lambda e, c=c, j=j, yv=yv: e.scalar_tensor_tensor(out=yv, in0=zx[:, c, 2:514], scalar=cwt[:, 2, j:j + 1], in1=yv, op0=ALU.mult, op1=ALU.add),
                             reads=[B_zx[c], B_ws, B_acc[0][c]], writes=[B_acc[0][c]])
                    sl = unit(4 + uu)
                    for c in range(4):
                        j = uu * 4 + c
                        pi = feat_mm(sl, c)
                        S.op("dve", lambda e, pi=pi, c=c, j=j: e.tensor_tensor(out=xT2[:, 8 + j, :], in0=psF[pi][:], in1=acc[:, 0, c * 512:(c + 1) * 512], op=ALU.mult),
                             reads=[B_psF[pi], B_acc[0][c]], writes=[B_xT2[2 + j // 4]])
                proj_tokmajor(wout_b[e_idx], B_w[f"wout{l}"], 4, xq_of_unit=lambda u: u)
                finish_tile(ti)

        def attn_mixer(l, src):
            o_idx = l // 2
            wb = wqkv_b[o_idx]
            wk = B_w[f"wqkv{l}"]
            load_gain(gbA, B_gbA, sem_g[0], n_mix_pre[l:l + 1, :])
            load_gain(gbB, B_gbB, sem_g[1], n_mix_post[l:l + 1, :])
            cosv = acc[:, 2, :].rearrange("p (t i) -> p t i", t=32)
            sinv = acc[:, 3, :].rearrange("p (t i) -> p t i", t=32)
            S.op("sp", lambda e: e.dma_start(out=cosv, in_=c_cos), writes=B_acc[2], dma_sem=sem_misc[6])
            S.op("sp", lambda e: e.dma_start(out=sinv, in_=c_sin), writes=B_acc[3], dma_sem=sem_misc[7])
            B_qk_d = Buf("qk_d")
            B_v_d = Buf("v_d")
            B_o_d = Buf("o_d")
            qi = 0
            for ti in range(NT):
                load_h(ti, src)
                norm_T()
                for cu in range(12):
                    sl = ring_load(wb[:, cu * 512:(cu + 1) * 512].rearrange("(k p) f -> p k f", p=128), v_k512, wk)
                    qb = qi % 2
                    qi += 1
                    for s in range(4):
                        pi = next_ps()
                        for kc in range(16):
                            S.op("pe", lambda e, pi=pi, sl=sl, kc=kc, s=s: e.matmul(psF[pi][:], lhsT=hnT[:, kc, s * 128:(s + 1) * 128], rhs=v_k512(ring[sl])[:, kc, :],
                                                                                 start=(kc == 0), stop=(kc == 15)),
                                 reads=[B_ring[sl], B_hnT], writes=[B_psF[pi]])
                        if cu < 8:
                            tt = ti * 4 + s
                            P4 = psF[pi][:].rearrange("p (h two i) -> p h two i", h=4, two=2)
                            A = acc[:, 0, (s % 2) * 1024:(s % 2) * 1024 + 512]
                            T = acc[:, 0, (s % 2) * 1024 + 512:(s % 2) * 1024 + 1024]
                            T4 = T.rearrange("p (h two i) -> p h two i", h=4, two=2)
                            bA, bT = B_acc[0][(s % 2) * 2], B_acc[0][(s % 2) * 2 + 1]
                            cb = cosv[:, tt, :]
                            sn = sinv[:, tt, :]
                            S.op("dve", lambda e, pi=pi, A=A, cb=cb: e.tensor_tensor(out=A.rearrange("p (h i) -> p h i", h=8), in0=psF[pi][:].rearrange("p (h i) -> p h i", h=8),
                                                                              in1=cb.unsqueeze(1).broadcast_to([128, 8, 64]), op=ALU.mult),
                                 reads=[B_psF[pi]] + B_acc[2], writes=[bA])
                            S.op("dve", lambda e, P4=P4, T4=T4, sn=sn: e.scalar_tensor_tensor(out=T4[:, :, 0, :], in0=P4[:, :, 1, :], scalar=-1.0, in1=sn.unsqueeze(1).broadcast_to([128, 4, 64]),
                                                                                        op0=ALU.mult, op1=ALU.mult),
                                 reads=[B_psF[pi]] + B_acc[3], writes=[bT])
                            S.op("dve", lambda e, P4=P4, T4=T4, sn=sn: e.tensor_tensor(out=T4[:, :, 1, :], in0=P4[:, :, 0, :], in1=sn.unsqueeze(1).broadcast_to([128, 4, 64]), op=ALU.mult),
                                 reads=[B_psF[pi]] + B_acc[3], writes=[bT])
                            S.op("pool", lambda e, A=A, T=T, qb=qb, s=s: e.tensor_tensor(out=qrb[qb][:, s, :], in0=A, in1=T, op=ALU.add),
                                 reads=[bA, bT], writes=[B_qrb[qb][s]])
                        else:
                            S.op("act", lambda e, pi=pi, qb=qb, s=s: e.activation(out=stg[qb][:, s, :], in_=psF[pi][:], func=AF.Copy),
                                 reads=[B_psF[pi]], writes=[B_stg[qb]])
                    if cu < 8:
                        for hh in range(4):
                            pt = state["pt"]
                            state["pt"] = 1 - pt
                            for s in range(4):
                                S.op("pe", lambda e, pt=pt, s=s, hh=hh, qb=qb: e.transpose(out=psT[pt][:, s * 128:(s + 1) * 128], in_=qrb[qb][:, s, hh * 128:(hh + 1) * 128], identity=identb[:]),
                                     reads=[B_qrb[qb][s], B_const], writes=[B_psT[pt]])
                            S.op("act", lambda e, pt=pt, hh=hh, qb=qb: e.activation(out=stg[qb][:, hh, :], in_=psT[pt][:, 0:512], func=AF.Copy),
                                 reads=[B_psT[pt]], writes=[B_stg[qb]])
                        dst = (qT_d if cu < 4 else kT_d)[(cu % 4) * 4:(cu % 4) * 4 + 4, :, ti * 512:(ti + 1) * 512].rearrange("h p t -> p h t")
                        S.op("sp", lambda e, dst=dst, qb=qb: e.dma_start(out=dst, in_=stg[qb][:]), reads=[B_stg[qb]], writes=[B_qk_d], dma_sem=sem_stg[qb])
                    else:
                        dst = v_d[ti * 512:(ti + 1) * 512, (cu - 8) * 512:(cu - 7) * 512].rearrange("(s p) c -> p s c", p=128)
                        S.op("sp", lambda e, dst=dst, qb=qb: e.dma_start(out=dst, in_=stg[qb][:]), reads=[B_stg[qb]], writes=[B_v_d], dma_sem=sem_stg[qb])
            S.barrier()
            QT = [ring[0][:, 0:4096], ring[0][:, 4096:8192]]
            KT = [ring[1][:, 0:4096], ring[1][:, 4096:8192]]
            VB = [ring[2][:, 0:4096].rearrange("p (b c) -> p b c", b=32), ring[2][:, 4096:8192].rearrange("p (b c) -> p b c", b=32)]
            OT = ring[3][:, 0:4096]
            B_QT = [Buf("QT0"), Buf("QT1")]
            B_KT = [Buf("KT0"), Buf("KT1")]
            B_VB = [Buf("VB0"), Buf("VB1")]
            B_OT = Buf("OT")
            B_ag = [Buf(f"ag{i}") for i in range(8)]
            accf = acc[:].rearrange("p (n a) d -> p n (a d)", n=2)
            scale = 1.0 / math.sqrt(128.0)
            vcount = 0
            for h in range(16):
                hb = h % 2
                S.op("sp", lambda e, h=h, hb=hb: e.dma_start(out=QT[hb], in_=qT_d[h]), reads=[B_qk_d], writes=[B_QT[hb]], dma_sem=sem_qk[hb])
                S.op("sp", lambda e, h=h, hb=hb: e.dma_start(out=KT[hb], in_=kT_d[h]), reads=[B_qk_d], writes=[B_KT[hb]], dma_sem=sem_qk[2 + hb])
                for br, d in enumerate((1, 4, 16)):
                    vb = vcount % 2
                    vcount += 1
                    nb = 32 // d
                    vsrc = v_d[:, h * 128:(h + 1) * 128].rearrange("(b p r) c -> r p b c", p=128, r=d)
                    for r in range(d):
                        S.op("sp", lambda e, r=r, vb=vb, nb=nb, vsrc=vsrc: e.dma_start(out=VB[vb][:, r * nb:(r + 1) * nb, :], in_=vsrc[r]),
                             reads=[B_v_d], writes=[B_VB[vb]] if r == 0 else [], dma_sem=sem_v[vb])
                    B_VB[vb].w = (sem_v[vb], S.cnt[sem_v[vb]])
                    for r in range(d):
                        for b in range(nb):
                            def cols(bb):
                                base = bb * 128 * d + r
                                return slice(base, base + 127 * d + 1, d) if d > 1 else slice(base, base + 128)
                            grp = {0: [b // 4], 1: [b], 2: list(range(4 * b, 4 * b + 4))}[br]
                            n = 256 if b > 0 else 128
                            ps_s = next_ps()
                            S.op("pe", lambda e, ps_s=ps_s, hb=hb, c=cols(b): e.matmul(psF[ps_s][:, 0:128], lhsT=KT[hb][:, c], rhs=QT[hb][:, c], start=True, stop=True),
                                 reads=[B_KT[hb], B_QT[hb]], writes=[B_psF[ps_s]])
                            if b > 0:
                                S.op("pe", lambda e, ps_s=ps_s, hb=hb, c=cols(b), cp=cols(b - 1): e.matmul(psF[ps_s][:, 128:256], lhsT=KT[hb][:, cp], rhs=QT[hb][:, c], start=True, stop=True),
                                     reads=[B_KT[hb], B_QT[hb]], writes=[B_psF[ps_s]])
                            ei = state["E"]
                            state["E"] = (ei + 1) % 4
                            S.op("act", lambda e, ps_s=ps_s, ei=ei, n=n: e.activation(out=Et[ei][:, 0:n], in_=psF[ps_s][:, 0:n], func=AF.Exp, scale=scale),
                                 reads=[B_psF[ps_s]], writes=[B_E[ei]])
                            S.op("pool", lambda e, ei=ei, n=n: e.tensor_tensor(out=Et[ei][:, 0:n], in0=Et[ei][:, 0:n], in1=maskb[:, 0:n], op=ALU.mult),
                                 reads=[B_E[ei], B_const], writes=[B_E[ei]])
                            ps_n = next_ps()
                            blk = r * nb + b
                            S.op("pe", lambda e, ps_n=ps_n, vb=vb, blk=blk, ei=ei, b=b: e.matmul(psF[ps_n][:, 0:128], lhsT=VB[vb][:, blk, :], rhs=Et[ei][:, 0:128], start=True, stop=(b == 0)),
                                 reads=[B_VB[vb], B_E[ei]], writes=[B_psF[ps_n]])
                            if b > 0:
                                S.op("pe", lambda e, ps_n=ps_n, vb=vb, blk=blk, ei=ei: e.matmul(psF[ps_n][:, 0:128], lhsT=VB[vb][:, blk - 1, :], rhs=Et[ei][:, 128:256], start=False, stop=True),
                                     reads=[B_VB[vb], B_E[ei]], writes=[B_psF[ps_n]])
                            S.op("pe", lambda e, ps_n=ps_n, ei=ei, b=b: e.matmul(psF[ps_n][:, 128:256], lhsT=onesb[:], rhs=Et[ei][:, 0:128], start=True, stop=(b == 0)),
                                 reads=[B_const, B_E[ei]], writes=[B_psF[ps_n]])
                            if b > 0:
                                S.op("pe", lambda e, ps_n=ps_n, ei=ei: e.matmul(psF[ps_n][:, 128:256], lhsT=onesb[:], rhs=Et[ei][:, 128:256], start=False, stop=True),
                                     reads=[B_const, B_E[ei]], writes=[B_psF[ps_n]])
                            gb = [B_ag[g_] for g_ in grp]
                            pv = psF[ps_n][:, 0:256].rearrange("p (n q) -> p n q", n=2)
                            if br == 0:
                                S.op("act", lambda e, pv=pv, c=cols(b): e.activation(out=accf[:, :, c], in_=pv, func=AF.Copy), reads=[B_psF[ps_n]], writes=gb)
                            else:
                                S.op("dve", lambda e, pv=pv, c=cols(b): e.tensor_tensor(out=accf[:, :, c], in0=pv, in1=accf[:, :, c], op=ALU.add),
                                     reads=[B_psF[ps_n]] + gb, writes=gb)
                S.op("dve", lambda e: e.reciprocal(out=accf[:, 1, :], in_=accf[:, 1, :]), reads=B_ag, writes=B_ag)
                S.op("pool", lambda e: e.tensor_tensor(out=OT, in0=accf[:, 0, :], in1=accf[:, 1, :], op=ALU.mult), reads=B_ag, writes=[B_OT] + B_ag)
                S.op("sp", lambda e, h=h: e.dma_start(out=oT_d[h], in_=OT), reads=[B_OT], writes=[B_o_d], dma_sem=sem_o)
            S.barrier()
            for ti in range(NT):
                load_h(ti, src)
                S.op("sp", lambda e, ti=ti: e.dma_start(out=xT2[:], in_=oT_d[:, :, ti * 512:(ti + 1) * 512].rearrange("h p t -> p h t")),
                     reads=[B_o_d], writes=B_xT2, dma_sem=sem_x2)
                proj_tokmajor(wo_b[o_idx], B_w[f"wo{l}"], 4, xq_of_unit=lambda u: u)
                finish_tile(ti)
            S.barrier()

        load_consts()
        cast_weights(layers[0])
        src = x
        for li, l in enumerate(layers):
            if "mix" in parts:
                if l % 2 == 0:
                    even_mixer(l, src)
                else:
                    attn_mixer(l, src)
                src = y
                S.barrier()
            if li + 1 < len(layers):
                cast_weights(layers[li + 1])
            if "mlp" in parts:
                mlp_layer(l, src)
                src = y
                S.barrier()
        S.emit(nc)
    return nc


def make_consts():
    ident = np.eye(128, dtype=np.float32)
    t = np.arange(128)
    tril = (t[None, :] <= t[:, None]).astype(np.float32)
    k = t[:, None]
    q = t[None, :]
    mask = np.concatenate([(k <= q), (k >= q)], axis=1).astype(np.float32)
    half = 64
    inv_freq = (10000.0 ** (-np.arange(half, dtype=np.float32) * 2.0 / 128)).astype(np.float32)
    pos = np.arange(SEQ, dtype=np.float32)
    ang = (pos[:, None] * inv_freq[None, :]).astype(np.float32)
    cos = np.cos(ang).astype(np.float32).reshape(32, 128, 64).transpose(1, 0, 2)
    sin = np.sin(ang).astype(np.float32).reshape(32, 128, 64).transpose(1, 0, 2)
    return {"c_ident": ident, "c_tril": tril, "c_mask": mask,
            "c_cos": np.ascontiguousarray(cos), "c_sin": np.ascontiguousarray(sin)}


_PROG_CACHE = {}


def run_layers(x, params, layers, trace=False, parts=("mix", "mlp")):
    key = (tuple(layers), tuple(parts))
    if key not in _PROG_CACHE:
        _PROG_CACHE[key] = build_program(layers, parts)
    nc = _PROG_CACHE[key]
    consts = make_consts()
    nb = x.shape[0]
    in_maps = []
    for b in range(nb):
        m = {"x": np.ascontiguousarray(x[b])}
        m.update(params)
        m.update(consts)
        in_maps.append(m)
    res = run_bass_kernel_spmd(nc, in_maps, core_ids=list(range(nb)), trace=trace)
    out = np.stack([res.results[b]["y"] for b in range(nb)], axis=0)
    return out, res


def kernel(x, norm_mix_pre, norm_mix_post, norm_mlp_pre, norm_mlp_post,
           w_in_ab, w_spatial, b_spatial, conv_w, w_out_ab, w_qkv, w_o, w_up, w_down):
    params = {
        "norm_mix_pre": norm_mix_pre, "norm_mix_post": norm_mix_post,
        "norm_mlp_pre": norm_mlp_pre, "norm_mlp_post": norm_mlp_post,
        "w_in_ab": w_in_ab, "w_spatial": w_spatial, "b_spatial": b_spatial, "conv_w": conv_w,
        "w_out_ab": w_out_ab, "w_qkv": w_qkv, "w_o": w_o, "w_up": w_up, "w_down": w_down,
    }
    params = {k: np.ascontiguousarray(np.asarray(v, dtype=np.float32)) for k, v in params.items()}
    x = np.asarray(x, dtype=np.float32)
    out, _ = run_layers(x, params, (0, 1, 2, 3))
    return out.astype(np.float32)
```

```python
import math
from contextlib import ExitStack

import numpy as np
import concourse.bass as bass
import concourse.mybir as mybir
from concourse.bass_utils import run_bass_kernel_spmd

F32 = mybir.dt.float32
BF16 = mybir.dt.bfloat16
AF = mybir.ActivationFunctionType
ALU = mybir.AluOpType

D = 2048
SEQ = 2048
NT = SEQ // 512
RG = [[0, 1], [2, 3], [4, 5], [6, 7]]
FFN = 8192
DEPTH = 4
ENGS = ("pe", "act", "dve", "pool", "sp")


class Buf:
    __slots__ = ("name", "w", "r")

    def __init__(self, name):
        self.name = name
        self.w = None
        self.r = []


class Op:
    __slots__ = ("eng", "fn", "waits", "token", "inc")

    def __init__(self, eng, fn):
        self.eng = eng
        self.fn = fn
        self.waits = []
        self.token = None


class Sched:
    def __init__(self):
        self.q = {e: [] for e in ENGS}
        self.cnt = {}
        self.waited = {e: {} for e in ENGS}
        self.dma_sems = []

    def dsem(self, name):
        self.dma_sems.append(name)
        self.cnt[name] = 0
        return name

    def op(self, eng, fn, reads=(), writes=(), dma_sem=None, inc=16):
        deps = set()
        for b in reads:
            if b.w is not None:
                deps.add(b.w)
        for b in writes:
            if b.w is not None:
                deps.add(b.w)
            deps.update(b.r)
        o = Op(eng, fn)
        if dma_sem is not None:
            key = dma_sem
            self.cnt[key] = self.cnt.get(key, 0) + inc
        else:
            key = eng
            self.cnt[key] = self.cnt.get(key, 0) + 1
        o.token = (key, self.cnt[key])
        o.inc = inc if dma_sem is not None else 1
        wd = self.waited[eng]
        for (k, v) in sorted(deps):
            if k == "pe" and eng == "pe":
                continue
            if wd.get(k, 0) >= v:
                continue
            wd[k] = v
            o.waits.append((k, v))
        for b in reads:
            b.r.append(o.token)
        for b in writes:
            b.w = o.token
            b.r = []
        self.q[eng].append(o)
        return o

    def barrier(self, engines=ENGS):
        snap = {k: v for k, v in self.cnt.items() if v > 0}
        for e in engines:
            o = Op(e, None)
            wd = self.waited[e]
            for k, v in sorted(snap.items()):
                if wd.get(k, 0) >= v:
                    continue
                wd[k] = v
                o.waits.append((k, v))
            self.q[e].append(o)

    def emit(self, nc):
        with ExitStack() as es:
            keys = list(ENGS) + list(self.dma_sems)
            sems = {k: es.enter_context(nc.semaphore("s_" + k)) for k in keys}
            block = es.enter_context(nc.Block())
            engmap = {"pe": block.tensor, "act": block.scalar, "dve": block.vector,
                      "pool": block.gpsimd, "sp": block.sync}

            def body_for(e):
                def body(eng):
                    for o in self.q[e]:
                        for (k, v) in o.waits:
                            eng.wait_ge(sems[k], v)
                        if o.fn is None:
                            continue
                        ins = o.fn(eng)
                        k, v = o.token
                        ins.then_inc(sems[k], o.inc)
                return body

            for e in ENGS:
                engmap[e](body_for(e))


def build_program(layers=(0, 1, 2, 3), parts=("mix", "mlp")):
    nc = bass.Bass("TRN2", target_bir_lowering=False)
    S = Sched()

    has_even = "mix" in parts and any(l % 2 == 0 for l in layers)
    has_odd = "mix" in parts and any(l % 2 == 1 for l in layers)
    has_mlp = "mlp" in parts
    needed = {"x", "c_ident", "c_tril", "c_mask", "c_cos", "c_sin", "c_flag"}
    if has_even:
        needed |= {"norm_mix_pre", "norm_mix_post", "w_in_ab", "w_spatial", "b_spatial", "conv_w", "w_out_ab"}
    if has_odd:
        needed |= {"norm_mix_pre", "norm_mix_post", "w_qkv", "w_o"}
    if has_mlp:
        needed |= {"norm_mlp_pre", "norm_mlp_post", "w_up", "w_down"}

    def din(name, shape):
        if name not in needed:
            return None
        return nc.dram_tensor(name, list(shape), F32, kind="ExternalInput").ap()

    x = din("x", (SEQ, D))
    n_mix_pre = din("norm_mix_pre", (DEPTH, D))
    n_mix_post = din("norm_mix_post", (DEPTH, D))
    n_mlp_pre = din("norm_mlp_pre", (DEPTH, D))
    n_mlp_post = din("norm_mlp_post", (DEPTH, D))
    w_in_ab = din("w_in_ab", (2, D, 5120))
    w_spatial = din("w_spatial", (2, 8, 128, 128))
    b_spatial = din("b_spatial", (2, 8, 128))
    conv_w = din("conv_w", (2, 3, 1024))
    w_out_ab = din("w_out_ab", (2, D, D))
    w_qkv = din("w_qkv", (2, D, 3 * D))
    w_o = din("w_o", (2, D, D))
    w_up = din("w_up", (DEPTH, D, FFN))
    w_down = din("w_down", (DEPTH, FFN, D))
    c_ident = din("c_ident", (128, 128))
    c_tril = din("c_tril", (128, 128))
    c_mask = din("c_mask", (128, 256))
    c_cos = din("c_cos", (128, 16, 64))
    c_sin = din("c_sin", (128, 16, 64))
    c_flag = din("c_flag", (128, 1))
    y = nc.dram_tensor("y", [SEQ, D], F32, kind="ExternalOutput").ap()

    def dscr(name, shape, dt=BF16):
        return nc.dram_tensor(name, list(shape), dt).ap()

    win_b = [dscr(f"win_b{e}", (D, 5120)) for e in range(2)]
    wout_b = [dscr(f"wout_b{e}", (D, D)) for e in range(2)]
    wqkv_b = [dscr(f"wqkv_b{e}", (D, 3 * D)) for e in range(2)]
    wo_b = [dscr(f"wo_b{e}", (D, D)) for e in range(2)]
    wup_b = [dscr(f"wup_b{l}", (D, FFN)) for l in range(DEPTH)]
    wdn_b = [dscr(f"wdn_b{l}", (FFN, D)) for l in range(DEPTH)]
    qT_d = dscr("qT_d", (D, SEQ))
    kT_d = dscr("kT_d", (D, SEQ))
    v_d = dscr("v_d", (SEQ, D))
    oT_d = dscr("oT_d", (D, SEQ))
    kTg = [dscr(f"kTg{i}", (1024, SEQ)) for i in range(4)]
    vg = [dscr(f"vg{i}", (1024, D)) for i in range(4)]
    zs_d = dscr("zs_d", (128, 64), F32)
    zg_d = dscr("zg_d", (256, 64), F32)

    es = ExitStack()
    with es:
        def sb(name, shape, dt):
            return es.enter_context(nc.sbuf_tensor(name, list(shape), dt))

        ht = sb("ht", (128, 4, D), F32)
        acc = sb("acc", (128, 4, D), F32)
        hnT = sb("hnT", (128, 16, 512), BF16)
        xT2 = sb("xT2", (128, 16, 512), BF16)
        ring = [sb(f"ring{i}", (128, 8192), BF16) for i in range(4)]
        gbA = sb("gbA", (128, D), F32)
        gbB = sb("gbB", (128, D), F32)
        _xsb = sb("xsb", (128, D), BF16)
        tmpA = _xsb
        xsb = [_xsb, _xsb]
        zxf = sb("zxf", (128, 2064), F32)
        zx = zxf[:].rearrange("p (c w) -> p c w", c=4)
        zh = sb("zh", (128, 8, 2), F32)
        zsend = sb("zsend", (128, 64), F32)
        zrecv = sb("zrecv", (128, 64), F32)
        flag = sb("flag", (128, 1), F32)
        st = sb("st", (128, 64), F32)
        identb = sb("identb", (128, 128), BF16)
        onesb = sb("onesb", (128, 128), BF16)
        maskb = sb("maskb", (128, 256), BF16)
        trilf = sb("trilf", (128, 128), F32)
        wsT = sb("wsT", (128, 8, 128), BF16)
        bsb = sb("bsb", (128, 8, 128), F32)
        cwt = sb("cwt", (128, 3, 8), F32)
        vq = sb("vq", (128, 4096), BF16)
        vn = vq[:].rearrange("p (s c) -> p s c", s=4)
        qrb = [vq[:, i * 2048:(i + 1) * 2048].rearrange("p (s c) -> p s c", s=4) for i in range(2)]
        stg = [zxf[:, i * 1024:(i + 1) * 1024].bitcast(BF16).rearrange("p (s c) -> p s c", s=4) for i in range(2)]
        relu_t = [zxf[:, i * 512:(i + 1) * 512] for i in range(2)]
        Et = [sb(f"Et{i}", (128, 256), BF16) for i in range(4)]

        psF = [es.enter_context(nc.psum_tensor(f"psF{i}", [128, 512], F32)) for i in range(6)]
        psT = [es.enter_context(nc.psum_tensor(f"psT{i}", [128, 1024], BF16)) for i in range(2)]

        B_ht = [Buf(f"ht{s}") for s in range(4)]
        B_acc = [[Buf(f"acc{s}_{c}") for c in range(4)] for s in range(4)]
        B_hnT = Buf("hnT")
        B_xT2 = [Buf(f"xT2_{i}") for i in range(4)]
        B_ring = [Buf(f"ring{i}") for i in range(4)]
        B_gbA, B_gbB = Buf("gbA"), Buf("gbB")
        _bx = Buf("xsb")
        B_xsb = [_bx, _bx]
        B_tmpA = _bx
        B_zx = [Buf(f"zx{i}") for i in range(4)]
        B_zh = Buf("zh")
        B_zs = Buf("zsend")
        B_cc = Buf("cc_chain")
        B_zr = Buf("zrecv")
        B_st = [Buf(f"st{i}") for i in range(16)]
        B_const = Buf("const")
        B_ws = Buf("ws")
        B_vn = [Buf(f"vn{s}") for s in range(4)]
        B_qrb = [[Buf(f"qrb{i}_{s}") for s in range(4)] for i in range(2)]
        B_stg = [Buf("stg0"), Buf("stg1")]
        B_relu = [Buf("relu0"), Buf("relu1")]
        B_E = [Buf(f"E{i}") for i in range(4)]
        B_psF = [Buf(f"psF{i}") for i in range(6)]
        B_psT = [Buf("psT0"), Buf("psT1")]
        B_y = Buf("y")
        B_ytile = [Buf(f"y{t}") for t in range(NT)]
        B_w = {}

        sem_ring = [S.dsem(f"dr{i}") for i in range(4)]
        sem_ht = S.dsem("dht")
        sem_st = S.dsem("dstore")
        sem_g = [S.dsem("dgA"), S.dsem("dgB")]
        sem_misc = [S.dsem(f"dm{i}") for i in range(8)]
        sem_cast = [S.dsem(f"dcast{i}") for i in range(DEPTH)]
        sem_stg = [S.dsem("dstg0"), S.dsem("dstg1")]
        sem_qk = [S.dsem(f"dqk{i}") for i in range(4)]
        sem_v = [S.dsem(f"dv{i}") for i in range(2)]
        sem_o = S.dsem("dot")
        sem_x2 = S.dsem("dx2")
        sem_cc = [S.dsem(f"dcc{i}") for i in range(3)]
        sem_cck = [S.dsem(f"dcck{i}") for i in range(4)]
        sem_ccv = [S.dsem(f"dccv{i}") for i in range(4)]
        sem_kp = [S.dsem("dkp0"), S.dsem("dkp1")]
        sem_z = [S.dsem("dz0"), S.dsem("dz1")]

        state = {"ps": 0, "pt": 0, "ring": 0, "st": 0, "relu": 0, "E": 0}

        def next_ps():
            i = state["ps"]
            state["ps"] = (i + 1) % 6
            return i

        def next_st():
            i = state["st"]
            state["st"] = (i + 1) % 16
            return i

        def load_consts():
            S.op("sp", lambda e: e.dma_start(out=acc[:, 0, 0:128], in_=c_ident), writes=[B_acc[0][0]], dma_sem=sem_misc[0])
            S.op("dve", lambda e: e.tensor_copy(out=identb[:], in_=acc[:, 0, 0:128]), reads=[B_acc[0][0]], writes=[B_const])
            S.op("sp", lambda e: e.dma_start(out=acc[:, 1, 0:256], in_=c_mask), writes=[B_acc[1][0]], dma_sem=sem_misc[1])
            S.op("dve", lambda e: e.tensor_copy(out=maskb[:], in_=acc[:, 1, 0:256]), reads=[B_acc[1][0]], writes=[B_const])
            S.op("dve", lambda e: e.memset(onesb[:], 1.0), writes=[B_const])
            S.op("sp", lambda e: e.dma_start(out=trilf[:], in_=c_tril), writes=[B_const], dma_sem=sem_misc[2])
            S.op("dve", lambda e: e.memset(zh[:], 0.0), writes=[B_zh])
            S.op("dve", lambda e: e.memset(zsend[:], 0.0), writes=[B_zs])
            S.op("sp", lambda e: e.dma_start(out=flag[:], in_=c_flag), writes=[B_const], dma_sem=sem_misc[3])

        def cast_weights(l):
            allb = []

            def cast(dst, src, key, nsplit):
                rows = src.shape[0]
                step = rows // nsplit
                bufs = []
                for i in range(nsplit):
                    b = Buf(f"{key}_{i}")
                    S.op("pool", lambda e, i=i: e.dma_start(out=dst[i * step:(i + 1) * step, :], in_=src[i * step:(i + 1) * step, :]),
                         writes=[b], dma_sem=sem_cast[l])
                    bufs.append(b)
                    allb.append(b)
                B_w[key] = bufs
            if True:
                if l % 2 == 0 and has_even:
                    cast(win_b[l // 2], w_in_ab[l // 2], f"win{l}", 2)
                    cast(wout_b[l // 2], w_out_ab[l // 2], f"wout{l}", 1)
                if l % 2 == 1 and has_odd:
                    cast(wqkv_b[l // 2], w_qkv[l // 2], f"wqkv{l}", 2)
                    cast(wo_b[l // 2], w_o[l // 2], f"wo{l}", 1)
                if has_mlp:
                    cast(wup_b[l], w_up[l], f"wup{l}", 2)
                    cast(wdn_b[l], w_down[l], f"wdn{l}", 2)
            for b in allb:
                b.w = (sem_cast[l], S.cnt[sem_cast[l]])

        def ring_load(src_ap, view_fn, wbufs):
            i = state["ring"]
            state["ring"] = (i + 1) % 4
            S.op("sp", lambda e: e.dma_start(out=view_fn(ring[i]), in_=src_ap), reads=wbufs, writes=[B_ring[i]], dma_sem=sem_ring[i])
            return i

        def v_k512(t):
            return t[:].rearrange("p (k f) -> p k f", k=16)

        def v_f2048(t):
            return t[:].rearrange("p (k f) -> p k f", k=4)

        def load_gain(dst, bdst, sem, vec_ap):
            S.op("sp", lambda e: e.dma_start(out=dst[:], in_=vec_ap.partition_broadcast(128)), writes=[bdst], dma_sem=sem)

        def load_h(ti, src):
            S.op("sp", lambda e: e.dma_start(out=ht[:], in_=src[ti * 512:(ti + 1) * 512, :].rearrange("(s p) d -> p s d", p=128)),
                 reads=[B_ytile[ti]], writes=B_ht, dma_sem=sem_ht)

        def rstd_from_ss(ci, n, eps):
            bs = B_st[ci // 4]
            S.op("dve", lambda e: e.tensor_scalar(out=st[:, ci + 1:ci + 2], in0=st[:, ci:ci + 1], scalar1=1.0 / n, scalar2=eps, op0=ALU.mult, op1=ALU.add),
                 reads=[bs], writes=[bs])
            S.op("act", lambda e: e.activation(out=st[:, ci + 1:ci + 2], in_=st[:, ci + 1:ci + 2], func=AF.Sqrt), reads=[bs], writes=[bs])
            S.op("dve", lambda e: e.reciprocal(out=st[:, ci + 2:ci + 3], in_=st[:, ci + 1:ci + 2]), reads=[bs], writes=[bs])

        def norm_T(subs=(0, 1, 2, 3)):
            for s in subs:
                si = next_st()
                ci = si * 4
                xb = s % 2
                S.op("act", lambda e, s=s, ci=ci: e.activation(out=tmpA[:], in_=ht[:, s, :], func=AF.Square, accum_out=st[:, ci:ci + 1]),
                     reads=[B_ht[s]], writes=[B_tmpA, B_st[si]])
                rstd_from_ss(ci, D, 1e-6)
                S.op("dve", lambda e, s=s, ci=ci, xb=xb: e.scalar_tensor_tensor(out=xsb[xb][:], in0=ht[:, s, :], scalar=st[:, ci + 2:ci + 3], in1=gbA[:], op0=ALU.mult, op1=ALU.mult),
                     reads=[B_ht[s], B_st[si], B_gbA], writes=[B_xsb[xb]])
                for half in range(2):
                    pt = state["pt"]
                    state["pt"] = 1 - pt
                    for j in range(8):
                        kc = half * 8 + j
                        S.op("pe", lambda e, pt=pt, j=j, kc=kc, xb=xb: e.transpose(out=psT[pt][:, j * 128:(j + 1) * 128], in_=xsb[xb][:, kc * 128:(kc + 1) * 128], identity=identb[:]),
                             reads=[B_xsb[xb], B_const], writes=[B_psT[pt]])
                    S.op("act", lambda e, pt=pt, half=half, s=s: e.activation(out=hnT[:, half * 8:(half + 1) * 8, s * 128:(s + 1) * 128],
                                                                          in_=psT[pt][:].rearrange("p (j t) -> p j t", j=8), func=AF.Copy),
                         reads=[B_psT[pt]], writes=[B_hnT])

        def finish_tile(ti):
            for s in range(4):
                si = next_st()
                ci = si * 4
                S.op("act", lambda e, s=s, ci=ci: e.activation(out=tmpA[:], in_=acc[:, s, :], func=AF.Square, accum_out=st[:, ci:ci + 1]),
                     reads=B_acc[s], writes=[B_tmpA, B_st[si]])
                rstd_from_ss(ci, D, 1e-6)
                S.op("dve", lambda e, s=s, ci=ci: e.scalar_tensor_tensor(out=acc[:, s, :], in0=acc[:, s, :], scalar=st[:, ci + 2:ci + 3], in1=gbB[:], op0=ALU.mult, op1=ALU.mult),
                     reads=B_acc[s] + [B_st[si], B_gbB], writes=B_acc[s])
                S.op("pool", lambda e, s=s: e.tensor_tensor(out=acc[:, s, :], in0=acc[:, s, :], in1=ht[:, s, :], op=ALU.add),
                     reads=B_acc[s] + [B_ht[s]], writes=B_acc[s])
            S.op("sp", lambda e: e.dma_start(out=y[ti * 512:(ti + 1) * 512, :].rearrange("(s p) d -> p s d", p=128), in_=acc[:]),
                 reads=[b for bs in B_acc for b in bs], writes=[B_ytile[ti]], dma_sem=sem_st)

        def proj_tokmajor(w_b, wkey_bufs, n_units, first_group_done=False, unit_row0=0, xq_of_unit=None, first=True):
            for g in range(n_units // 2):
                slots = []
                for uu in range(2):
                    u = g * 2 + uu
                    r0 = unit_row0 + u * 512
                    slots.append(ring_load(w_b[r0:r0 + 512, :].rearrange("(k p) d -> p k d", p=128), v_f2048, wkey_bufs))
                for s in range(4):
                    for dc in range(4):
                        pi = next_ps()
                        for uu in range(2):
                            u = g * 2 + uu
                            xq = xq_of_unit(u)
                            for j in range(4):
                                S.op("pe", lambda e, pi=pi, xq=xq, j=j, s=s, dc=dc, sl=slots[uu], uu=uu: e.matmul(
                                    psF[pi][:], lhsT=xT2[:, xq * 4 + j, s * 128:(s + 1) * 128], rhs=v_f2048(ring[sl])[:, j, dc * 512:(dc + 1) * 512],
                                    start=(uu == 0 and j == 0), stop=(uu == 1 and j == 3)),
                                    reads=[B_xT2[xq], B_ring[slots[uu]]], writes=[B_psF[pi]])
                        if first and g == 0:
                            S.op("act", lambda e, pi=pi, s=s, dc=dc: e.activation(out=acc[:, s, dc * 512:(dc + 1) * 512], in_=psF[pi][:], func=AF.Copy),
                                 reads=[B_psF[pi]], writes=[B_acc[s][dc]])
                        else:
                            S.op("dve", lambda e, pi=pi, s=s, dc=dc: e.tensor_tensor(out=acc[:, s, dc * 512:(dc + 1) * 512], in0=psF[pi][:], in1=acc[:, s, dc * 512:(dc + 1) * 512], op=ALU.add),
                                 reads=[B_psF[pi], B_acc[s][dc]], writes=[B_acc[s][dc]])

        def mlp_layer(l, src):
            load_gain(gbA, B_gbA, sem_g[0], n_mlp_pre[l:l + 1, :])
            load_gain(gbB, B_gbB, sem_g[1], n_mlp_post[l:l + 1, :])
            for ti in range(NT):
                load_h(ti, src)
                norm_T()
                for g in range(8):
                    ub = g % 2
                    for uu in range(2):
                        c0 = (g * 2 + uu) * 512
                        sl = ring_load(wup_b[l][:, c0:c0 + 512].rearrange("(k p) f -> p k f", p=128), v_k512, B_w[f"wup{l}"])
                        for f in range(4):
                            pi = next_ps()
                            for kc in range(16):
                                S.op("pe", lambda e, pi=pi, sl=sl, kc=kc, f=f: e.matmul(psF[pi][:], lhsT=v_k512(ring[sl])[:, kc, f * 128:(f + 1) * 128], rhs=hnT[:, kc, :],
                                                                                     start=(kc == 0), stop=(kc == 15)),
                                     reads=[B_ring[sl], B_hnT], writes=[B_psF[pi]])
                            ri = state["relu"]
                            state["relu"] = 1 - ri
                            xq = ub * 2 + uu
                            S.op("act", lambda e, pi=pi, ri=ri: e.activation(out=relu_t[ri][:], in_=psF[pi][:], func=AF.Relu),
                                 reads=[B_psF[pi]], writes=[B_relu[ri]])
                            S.op("dve", lambda e, ri=ri, xq=xq, f=f: e.tensor_tensor(out=xT2[:, xq * 4 + f, :], in0=relu_t[ri][:], in1=relu_t[ri][:], op=ALU.mult),
                                 reads=[B_relu[ri]], writes=[B_xT2[xq]])
                    proj_tokmajor(wdn_b[l], B_w[f"wdn{l}"], 2, unit_row0=g * 1024, xq_of_unit=lambda u, ub=ub: ub * 2 + u, first=(g == 0))
                finish_tile(ti)

        def prep_even(e_idx):
            S.op("sp", lambda e: e.dma_start(out=acc[:, 0, 0:1024].rearrange("p (g s) -> p g s", g=8), in_=w_spatial[e_idx].rearrange("g t s -> t g s")),
                 writes=B_acc[0], dma_sem=sem_misc[3])
            S.op("dve", lambda e: e.tensor_tensor(out=xsb[0][:, 0:1024].rearrange("p (g s) -> p g s", g=8), in0=acc[:, 0, 0:1024].rearrange("p (g s) -> p g s", g=8),
                                                  in1=trilf[:].unsqueeze(1).broadcast_to([128, 8, 128]), op=ALU.mult),
                 reads=B_acc[0] + [B_const], writes=[B_xsb[0]])
            for g in range(8):
                S.op("pe", lambda e, g=g: e.transpose(out=psT[0][:, g * 128:(g + 1) * 128], in_=xsb[0][:, g * 128:(g + 1) * 128], identity=identb[:]),
                     reads=[B_xsb[0], B_const], writes=[B_psT[0]])
            S.op("act", lambda e: e.activation(out=wsT[:], in_=psT[0][:].rearrange("p (g t) -> p g t", g=8), func=AF.Copy), reads=[B_psT[0]], writes=[B_ws])
            S.op("sp", lambda e: e.dma_start(out=bsb[:].rearrange("p g t -> p (g t)"), in_=b_spatial[e_idx:e_idx + 1].rearrange("o g t -> o (g t)").partition_broadcast(128)),
                 writes=[B_ws], dma_sem=sem_misc[4])
            for k in range(3):
                S.op("sp", lambda e, k=k: e.dma_start(out=cwt[:, k, :], in_=conv_w[e_idx, k].rearrange("(j p) -> p j", p=128), allow_slow_non_contiguous=True),
                     writes=[B_ws], dma_sem=sem_misc[5])

        def even_mixer(l, src):
            e_idx = l // 2
            wb = win_b[e_idx]
            wk = B_w[f"win{l}"]
            prep_even(e_idx)
            load_gain(gbA, B_gbA, sem_g[0], n_mix_pre[l:l + 1, :])
            load_gain(gbB, B_gbB, sem_g[1], n_mix_post[l:l + 1, :])

            def unit(u):
                return ring_load(wb[:, u * 512:(u + 1) * 512].rearrange("(k p) f -> p k f", p=128), v_k512, wk)

            def feat_mm(sl, c, t0=0, t1=512):
                pi = next_ps()
                for kc in range(16):
                    S.op("pe", lambda e, pi=pi, sl=sl, kc=kc, c=c: e.matmul(psF[pi][:, 0:t1 - t0], lhsT=v_k512(ring[sl])[:, kc, c * 128:(c + 1) * 128], rhs=hnT[:, kc, t0:t1],
                                                                         start=(kc == 0), stop=(kc == 15)),
                         reads=[B_ring[sl], B_hnT], writes=[B_psF[pi]])
                return pi

            S.op("sp", lambda e: e.dma_start(out=ht[:, 3, :], in_=src[SEQ - 128:SEQ, :]), reads=[B_ytile[NT - 1]], writes=[B_ht[3]], dma_sem=sem_ht)
            norm_T(subs=(3,))
            for uu in range(2):
                sl = unit(6 + uu)
                for c in range(4):
                    pi = feat_mm(sl, c, 384, 512)
                    S.op("act", lambda e, pi=pi, c=c: e.activation(out=acc[:, 0, c * 512:c * 512 + 128], in_=psF[pi][:, 0:128], func=AF.Copy),
                         reads=[B_psF[pi]], writes=[B_acc[0][c]])
                sl = unit(8 + uu)
                for c in range(4):
                    j = uu * 4 + c
                    pi = feat_mm(sl, c, 384, 512)
                    S.op("dve", lambda e, pi=pi, c=c, j=j: e.tensor_tensor(out=zsend[:, 2 * j:2 * j + 2], in0=psF[pi][:, 126:128], in1=acc[:, 0, c * 512 + 126:c * 512 + 128], op=ALU.mult),
                         reads=[B_psF[pi], B_acc[0][c]], writes=[B_zs])
            B_zsd, B_zgd = Buf("zs_d"), Buf("zg_d")
            S.op("sp", lambda e: e.dma_start(out=zs_d, in_=zsend[:]), reads=[B_zs], writes=[B_zsd], dma_sem=sem_z[0])
            S.op("pool", lambda e: e.collective_compute("AllGather", ALU.bypass, replica_groups=RG, ins=[zs_d.opt()], outs=[zg_d.opt()]),
                 reads=[B_zsd], writes=[B_zgd, B_cc], dma_sem=sem_cc[2], inc=1)
            S.op("sp", lambda e: e.dma_start(out=zrecv[:], in_=zg_d[0:128, :]), reads=[B_zgd], writes=[B_zr], dma_sem=sem_z[1])
            S.op("dve", lambda e: e.tensor_scalar(out=zh[:].rearrange("p j k -> p (j k)"), in0=zrecv[:, 0:16], scalar1=flag[:, 0:1], scalar2=0.0, op0=ALU.mult, op1=ALU.add),
                 reads=[B_zr, B_const], writes=[B_zh])

            for ti in range(NT):
                load_h(ti, src)
                norm_T()
                sl_av = [unit(2), unit(3)]
                for s in range(4):
                    for hf in range(2):
                        pi = next_ps()
                        for kc in range(16):
                            S.op("pe", lambda e, pi=pi, sl=sl_av[hf], kc=kc, s=s: e.matmul(psF[pi][:], lhsT=hnT[:, kc, s * 128:(s + 1) * 128], rhs=v_k512(ring[sl])[:, kc, :],
                                                                                 start=(kc == 0), stop=(kc == 15)),
                                 reads=[B_ring[sl_av[hf]], B_hnT], writes=[B_psF[pi]])
                        S.op("act", lambda e, pi=pi, hf=hf, s=s: e.activation(out=acc[:, 3, (s % 2) * 1024 + hf * 512:(s % 2) * 1024 + (hf + 1) * 512], in_=psF[pi][:], func=AF.Gelu_apprx_tanh),
                             reads=[B_psF[pi]], writes=[B_acc[3][(s % 2) * 2 + hf]])
                    avs = acc[:, 3, (s % 2) * 1024:(s % 2 + 1) * 1024]
                    bav = [B_acc[3][(s % 2) * 2], B_acc[3][(s % 2) * 2 + 1]]
                    si = next_st()
                    ci = si * 4
                    S.op("act", lambda e, avs=avs, ci=ci: e.activation(out=tmpA[:, 0:1024], in_=avs, func=AF.Copy, accum_out=st[:, ci:ci + 1]),
                         reads=bav, writes=[B_tmpA, B_st[si]])
                    S.op("act", lambda e, avs=avs, ci=ci: e.activation(out=tmpA[:, 1024:2048], in_=avs, func=AF.Square, accum_out=st[:, ci + 1:ci + 2]),
                         reads=bav, writes=[B_tmpA, B_st[si]])
                    S.op("dve", lambda e, ci=ci: e.tensor_scalar(out=st[:, ci:ci + 1], in0=st[:, ci:ci + 1], scalar1=1.0 / 1024, scalar2=0.0, op0=ALU.mult, op1=ALU.add),
                         reads=[B_st[si]], writes=[B_st[si]])
                    S.op("dve", lambda e, ci=ci: e.tensor_tensor(out=st[:, ci + 2:ci + 3], in0=st[:, ci:ci + 1], in1=st[:, ci:ci + 1], op=ALU.mult),
                         reads=[B_st[si]], writes=[B_st[si]])
                    S.op("dve", lambda e, ci=ci: e.scalar_tensor_tensor(out=st[:, ci + 1:ci + 2], in0=st[:, ci + 1:ci + 2], scalar=1.0 / 1024, in1=st[:, ci + 2:ci + 3], op0=ALU.mult, op1=ALU.subtract),
                         reads=[B_st[si]], writes=[B_st[si]])
                    S.op("dve", lambda e, ci=ci: e.tensor_scalar(out=st[:, ci + 1:ci + 2], in0=st[:, ci + 1:ci + 2], scalar1=1e-5, scalar2=0.0, op0=ALU.add, op1=ALU.add),
                         reads=[B_st[si]], writes=[B_st[si]])
                    S.op("act", lambda e, ci=ci: e.activation(out=st[:, ci + 1:ci + 2], in_=st[:, ci + 1:ci + 2], func=AF.Sqrt), reads=[B_st[si]], writes=[B_st[si]])
                    S.op("dve", lambda e, ci=ci: e.reciprocal(out=st[:, ci + 2:ci + 3], in_=st[:, ci + 1:ci + 2]), reads=[B_st[si]], writes=[B_st[si]])
                    S.op("dve", lambda e, avs=avs, ci=ci, s=s: e.tensor_scalar(out=vn[:, s, :], in0=avs, scalar1=st[:, ci:ci + 1], scalar2=st[:, ci + 2:ci + 3], op0=ALU.subtract, op1=ALU.mult),
                         reads=bav + [B_st[si]], writes=[B_vn[s]])
                for uu in range(2):
                    sl = unit(uu)
                    for c in range(4):
                        j = uu * 4 + c
                        pi = feat_mm(sl, c)
                        S.op("act", lambda e, pi=pi, c=c: e.activation(out=acc[:, 1, c * 512:(c + 1) * 512], in_=psF[pi][:], func=AF.Gelu_apprx_tanh),
                             reads=[B_psF[pi]], writes=[B_acc[1][c]])
                        pm = next_ps()
                        for s in range(4):
                            S.op("pe", lambda e, pm=pm, s=s, j=j: e.matmul(psF[pm][:, s * 128:(s + 1) * 128], lhsT=vn[:, s, j * 128:(j + 1) * 128], rhs=wsT[:, j, :], start=True, stop=True),
                                 reads=[B_vn[s], B_ws], writes=[B_psF[pm]])
                        S.op("dve", lambda e, pm=pm, c=c, j=j: e.tensor_tensor(out=acc[:, 2, c * 512:(c + 1) * 512].rearrange("p (s t) -> p s t", s=4), in0=psF[pm][:].rearrange("p (s t) -> p s t", s=4),
                                                                       in1=bsb[:, j, :].unsqueeze(1).broadcast_to([128, 4, 128]), op=ALU.add),
                             reads=[B_psF[pm], B_ws], writes=[B_acc[2][c]])
                        S.op("pool", lambda e, c=c, j=j: e.tensor_tensor(out=xT2[:, j, :], in0=acc[:, 2, c * 512:(c + 1) * 512], in1=acc[:, 1, c * 512:(c + 1) * 512], op=ALU.mult),
                             reads=[B_acc[2][c], B_acc[1][c]], writes=[B_xT2[j // 4]])
                for uu in range(2):
                    sl = unit(6 + uu)
                    for c in range(4):
                        pi = feat_mm(sl, c)
                        S.op("act", lambda e, pi=pi, c=c: e.activation(out=acc[:, 0, c * 512:(c + 1) * 512], in_=psF[pi][:], func=AF.Copy),
                             reads=[B_psF[pi]], writes=[B_acc[0][c]])
                    sl = unit(8 + uu)
                    for c in range(4):
                        j = uu * 4 + c
                        pi = feat_mm(sl, c)
                        S.op("dve", lambda e, pi=pi, c=c: e.tensor_tensor(out=zx[:, c, 2:514], in0=psF[pi][:], in1=acc[:, 0, c * 512:(c + 1) * 512], op=ALU.mult),
                             reads=[B_psF[pi], B_acc[0][c]], writes=[B_zx[c]])
                        S.op("pool", lambda e, c=c, j=j: e.tensor_copy(out=zx[:, c, 0:2], in_=zh[:, j, :]), reads=[B_zh], writes=[B_zx[c]])
                        S.op("pool", lambda e, c=c, j=j: e.tensor_copy(out=zh[:, j, :], in_=zx[:, c, 512:514]), reads=[B_zx[c]], writes=[B_zh])
                        yv = acc[:, 0, c * 512:(c + 1) * 512]
                        S.op("pool", lambda e, c=c, j=j, yv=yv: e.tensor_scalar(out=yv, in0=zx[:, c, 0:512], scalar1=cwt[:, 0, j:j + 1], scalar2=0.0, op0=ALU.mult, op1=ALU.add),
                             reads=[B_zx[c], B_ws], writes=[B_acc[0][c]])
                        S.op("dve", lambda e, c=c, j=j, yv=yv: e.scalar_tensor_tensor(out=yv, in0=zx[:, c, 1:513], scalar=cwt[:, 1, j:j + 1], in1=yv, op0=ALU.mult, op1=ALU.add),
                             reads=[B_zx[c], B_ws, B_acc[0][c]], writes=[B_acc[0][c]])
                        S.op("dve", lambda e, c=c, j=j, yv=yv: e.scalar_tensor_tensor(out=yv, in0=zx[:, c, 2:514], scalar=cwt[:, 2, j:j + 1], in1=yv, op0=ALU.mult, op1=ALU.add),
                             reads=[B_zx[c], B_ws, B_acc[0][c]], writes=[B_acc[0][c]])
                    sl = unit(4 + uu)
                    for c in range(4):
                        j = uu * 4 + c
                        pi = feat_mm(sl, c)
                        S.op("dve", lambda e, pi=pi, c=c, j=j: e.tensor_tensor(out=xT2[:, 8 + j, :], in0=psF[pi][:], in1=acc[:, 0, c * 512:(c + 1) * 512], op=ALU.mult),
                             reads=[B_psF[pi], B_acc[0][c]], writes=[B_xT2[2 + j // 4]])
                proj_tokmajor(wout_b[e_idx], B_w[f"wout{l}"], 4, xq_of_unit=lambda u: u)
                finish_tile(ti)

        def attn_mixer(l, src):
            o_idx = l // 2
            wb = wqkv_b[o_idx]
            wk = B_w[f"wqkv{l}"]
            load_gain(gbA, B_gbA, sem_g[0], n_mix_pre[l:l + 1, :])
            load_gain(gbB, B_gbB, sem_g[1], n_mix_post[l:l + 1, :])
            cosv = acc[:, 2, 0:1024].rearrange("p (t i) -> p t i", t=16)
            sinv = acc[:, 3, 0:1024].rearrange("p (t i) -> p t i", t=16)
            S.op("sp", lambda e: e.dma_start(out=cosv, in_=c_cos), writes=B_acc[2], dma_sem=sem_misc[6])
            S.op("sp", lambda e: e.dma_start(out=sinv, in_=c_sin), writes=B_acc[3], dma_sem=sem_misc[7])
            B_qk_d = Buf("qk_d")
            B_v_d = Buf("v_d")
            B_o_d = Buf("o_d")
            qi = 0
            for ti in range(NT):
                load_h(ti, src)
                norm_T()
                for cu in range(12):
                    sl = ring_load(wb[:, cu * 512:(cu + 1) * 512].rearrange("(k p) f -> p k f", p=128), v_k512, wk)
                    qb = qi % 2
                    qi += 1
                    for s in range(4):
                        pi = next_ps()
                        for kc in range(16):
                            S.op("pe", lambda e, pi=pi, sl=sl, kc=kc, s=s: e.matmul(psF[pi][:], lhsT=hnT[:, kc, s * 128:(s + 1) * 128], rhs=v_k512(ring[sl])[:, kc, :],
                                                                                 start=(kc == 0), stop=(kc == 15)),
                                 reads=[B_ring[sl], B_hnT], writes=[B_psF[pi]])
                        if cu < 8:
                            tt = ti * 4 + s
                            P4 = psF[pi][:].rearrange("p (h two i) -> p h two i", h=4, two=2)
                            A = acc[:, 0, (s % 2) * 1024:(s % 2) * 1024 + 512]
                            T = acc[:, 0, (s % 2) * 1024 + 512:(s % 2) * 1024 + 1024]
                            T4 = T.rearrange("p (h two i) -> p h two i", h=4, two=2)
                            bA, bT = B_acc[0][(s % 2) * 2], B_acc[0][(s % 2) * 2 + 1]
                            cb = cosv[:, tt, :]
                            sn = sinv[:, tt, :]
                            S.op("dve", lambda e, pi=pi, A=A, cb=cb: e.tensor_tensor(out=A.rearrange("p (h i) -> p h i", h=8), in0=psF[pi][:].rearrange("p (h i) -> p h i", h=8),
                                                                              in1=cb.unsqueeze(1).broadcast_to([128, 8, 64]), op=ALU.mult),
                                 reads=[B_psF[pi]] + B_acc[2], writes=[bA])
                            S.op("dve", lambda e, P4=P4, T4=T4, sn=sn: e.scalar_tensor_tensor(out=T4[:, :, 0, :], in0=P4[:, :, 1, :], scalar=-1.0, in1=sn.unsqueeze(1).broadcast_to([128, 4, 64]),
                                                                                        op0=ALU.mult, op1=ALU.mult),
                                 reads=[B_psF[pi]] + B_acc[3], writes=[bT])
                            S.op("dve", lambda e, P4=P4, T4=T4, sn=sn: e.tensor_tensor(out=T4[:, :, 1, :], in0=P4[:, :, 0, :], in1=sn.unsqueeze(1).broadcast_to([128, 4, 64]), op=ALU.mult),
                                 reads=[B_psF[pi]] + B_acc[3], writes=[bT])
                            S.op("pool", lambda e, A=A, T=T, qb=qb, s=s: e.tensor_tensor(out=qrb[qb][:, s, :], in0=A, in1=T, op=ALU.add),
                                 reads=[bA, bT], writes=[B_qrb[qb][s]])
                        else:
                            S.op("act", lambda e, pi=pi, qb=qb, s=s: e.activation(out=stg[qb][:, s, :], in_=psF[pi][:], func=AF.Copy),
                                 reads=[B_psF[pi]], writes=[B_stg[qb]])
                    if cu < 8:
                        for hh in range(4):
                            pt = state["pt"]
                            state["pt"] = 1 - pt
                            for s in range(4):
                                S.op("pe", lambda e, pt=pt, s=s, hh=hh, qb=qb: e.transpose(out=psT[pt][:, s * 128:(s + 1) * 128], in_=qrb[qb][:, s, hh * 128:(hh + 1) * 128], identity=identb[:]),
                                     reads=[B_qrb[qb][s], B_const], writes=[B_psT[pt]])
                            S.op("act", lambda e, pt=pt, hh=hh, qb=qb: e.activation(out=stg[qb][:, hh, :], in_=psT[pt][:, 0:512], func=AF.Copy),
                                 reads=[B_psT[pt]], writes=[B_stg[qb]])
                        dst = (qT_d if cu < 4 else kT_d)[(cu % 4) * 512:(cu % 4) * 512 + 512, ti * 512:(ti + 1) * 512].rearrange("(h p) t -> p h t", p=128)
                        S.op("sp", lambda e, dst=dst, qb=qb: e.dma_start(out=dst, in_=stg[qb][:]), reads=[B_stg[qb]], writes=[], dma_sem=sem_stg[qb])
                    else:
                        dst = v_d[ti * 512:(ti + 1) * 512, (cu - 8) * 512:(cu - 7) * 512].rearrange("(s p) c -> p s c", p=128)
                        S.op("sp", lambda e, dst=dst, qb=qb: e.dma_start(out=dst, in_=stg[qb][:]), reads=[B_stg[qb]], writes=[], dma_sem=sem_stg[qb])
            S.barrier()
            B_kg = [Buf(f"kTg{i}") for i in range(4)]
            B_vg = [Buf(f"vg{i}") for i in range(4)]

            def gather_k(i):
                S.op("pool", lambda e: e.collective_compute("AllGather", ALU.bypass, replica_groups=RG, ins=[kT_d[i * 512:(i + 1) * 512, :].opt()], outs=[kTg[i].opt()]),
                     reads=[B_qk_d], writes=[B_kg[i], B_cc], dma_sem=sem_cck[i], inc=1)

            def gather_v(i):
                S.op("pool", lambda e: e.collective_compute("AllGather", ALU.bypass, replica_groups=RG, ins=[v_d[i * 512:(i + 1) * 512, :].opt()], outs=[vg[i].opt()]),
                     reads=[B_v_d], writes=[B_vg[i], B_cc], dma_sem=sem_ccv[i], inc=1)
            gather_k(0)
            gather_v(3)
            for i in range(3):
                gather_v(i)
            for i in range(1, 4):
                gather_k(i)
            QT = [ring[0][:, 0:2048], ring[0][:, 2048:4096]]
            KT = [ring[0][:, 4096:6144], ring[0][:, 6144:8192]]
            KPT = [ring[1][:, 0:2048], ring[1][:, 2048:4096]]
            OT = ring[1][:, 4096:6144]
            VB = [ring[2][:, 0:4096].rearrange("p (b c) -> p b c", b=32), ring[2][:, 4096:8192].rearrange("p (b c) -> p b c", b=32)]
            B_QT = [Buf("QT0"), Buf("QT1")]
            B_KT = [Buf("KT0"), Buf("KT1")]
            B_KPT = [Buf("KPT0"), Buf("KPT1")]
            B_VB = [Buf("VB0"), Buf("VB1")]
            B_OT = Buf("OT")
            B_ag = [Buf(f"ag{i}") for i in range(4)]
            accf = acc[:, 0:2, :]
            scale = 1.0 / math.sqrt(128.0)
            vcount = 0
            for h in range(16):
                hb = h % 2
                hr = slice(h * 128, (h + 1) * 128)
                S.op("sp", lambda e, hr=hr, hb=hb: e.dma_start(out=QT[hb], in_=qT_d[hr, :]), reads=[B_qk_d], writes=[B_QT[hb]], dma_sem=sem_qk[hb])
                S.op("sp", lambda e, hr=hr, hb=hb: e.dma_start(out=KT[hb], in_=kT_d[hr, :]), reads=[B_qk_d], writes=[B_KT[hb]], dma_sem=sem_qk[2 + hb])
                S.op("sp", lambda e, h=h, hb=hb: e.dma_start(out=KPT[hb], in_=kTg[h // 4][(h % 4) * 128:(h % 4 + 1) * 128, :]), reads=[B_kg[h // 4]], writes=[B_KPT[hb]], dma_sem=sem_kp[hb])
                for br, d in enumerate((1, 4, 16)):
                    vb = vcount % 2
                    vcount += 1
                    nb = 16 // d
                    vsrc = v_d[:, hr].rearrange("(b p r) c -> r p b c", p=128, r=d)
                    for r in range(d):
                        S.op("sp", lambda e, r=r, vb=vb, nb=nb, vsrc=vsrc: e.dma_start(out=VB[vb][:, r * nb:(r + 1) * nb, :], in_=vsrc[r]),
                             reads=[B_v_d], writes=[B_VB[vb]] if r == 0 else [], dma_sem=sem_v[vb])
                    if d == 1:
                        S.op("sp", lambda e, vb=vb, hr=hr: e.dma_start(out=VB[vb][:, 16, :], in_=vg[3][384:512, hr]), reads=[B_vg[3]], writes=[], dma_sem=sem_v[vb])
                    elif d == 4:
                        S.op("sp", lambda e, vb=vb, hr=hr: e.dma_start(out=VB[vb][:, 16:20, :], in_=vg[3][0:512, hr].rearrange("(p r) c -> p r c", r=4)),
                             reads=[B_vg[3]], writes=[], dma_sem=sem_v[vb])
                    else:
                        for j in range(4):
                            S.op("sp", lambda e, vb=vb, hr=hr, j=j: e.dma_start(out=VB[vb][32 * j:32 * j + 32, 16:32, :], in_=vg[j][0:512, hr].rearrange("(p r) c -> p r c", r=16)),
                                 reads=[B_vg[j]], writes=[], dma_sem=sem_v[vb])
                    B_VB[vb].w = (sem_v[vb], S.cnt[sem_v[vb]])
                    for r in range(d):
                        for b in range(nb):
                            def cols(bb):
                                base = bb * 128 * d + r
                                return slice(base, base + 127 * d + 1, d) if d > 1 else slice(base, base + 128)
                            grp = {0: [b // 4], 1: [b], 2: [0, 1, 2, 3]}[br]
                            ps_s = next_ps()
                            S.op("pe", lambda e, ps_s=ps_s, hb=hb, c=cols(b): e.matmul(psF[ps_s][:, 0:128], lhsT=KT[hb][:, c], rhs=QT[hb][:, c], start=True, stop=True),
                                 reads=[B_KT[hb], B_QT[hb]], writes=[B_psF[ps_s]])
                            if b > 0:
                                S.op("pe", lambda e, ps_s=ps_s, hb=hb, c=cols(b), cp=cols(b - 1): e.matmul(psF[ps_s][:, 128:256], lhsT=KT[hb][:, cp], rhs=QT[hb][:, c], start=True, stop=True),
                                     reads=[B_KT[hb], B_QT[hb]], writes=[B_psF[ps_s]])
                            else:
                                S.op("pe", lambda e, ps_s=ps_s, hb=hb, c=cols(b), cp=cols(nb - 1): e.matmul(psF[ps_s][:, 128:256], lhsT=KPT[hb][:, cp], rhs=QT[hb][:, c], start=True, stop=True),
                                     reads=[B_KPT[hb], B_QT[hb]], writes=[B_psF[ps_s]])
                            ei = state["E"]
                            state["E"] = (ei + 1) % 4
                            S.op("act", lambda e, ps_s=ps_s, ei=ei: e.activation(out=Et[ei][:], in_=psF[ps_s][:, 0:256], func=AF.Exp, scale=scale),
                                 reads=[B_psF[ps_s]], writes=[B_E[ei]])
                            S.op("pool", lambda e, ei=ei: e.tensor_tensor(out=Et[ei][:], in0=Et[ei][:], in1=maskb[:], op=ALU.mult),
                                 reads=[B_E[ei], B_const], writes=[B_E[ei]])
                            if b == 0:
                                S.op("pool", lambda e, ei=ei: e.tensor_scalar(out=Et[ei][:, 128:256], in0=Et[ei][:, 128:256], scalar1=flag[:, 0:1], scalar2=0.0, op0=ALU.mult, op1=ALU.add),
                                     reads=[B_E[ei], B_const], writes=[B_E[ei]])
                            ps_n = next_ps()
                            blk = r * nb + b
                            pblk = blk - 1 if b > 0 else 16 + r
                            S.op("pe", lambda e, ps_n=ps_n, vb=vb, blk=blk, ei=ei: e.matmul(psF[ps_n][:, 0:128], lhsT=VB[vb][:, blk, :], rhs=Et[ei][:, 0:128], start=True, stop=False),
                                 reads=[B_VB[vb], B_E[ei]], writes=[B_psF[ps_n]])
                            S.op("pe", lambda e, ps_n=ps_n, vb=vb, pblk=pblk, ei=ei: e.matmul(psF[ps_n][:, 0:128], lhsT=VB[vb][:, pblk, :], rhs=Et[ei][:, 128:256], start=False, stop=True),
                                 reads=[B_VB[vb], B_E[ei]], writes=[B_psF[ps_n]])
                            S.op("pe", lambda e, ps_n=ps_n, ei=ei: e.matmul(psF[ps_n][:, 128:256], lhsT=onesb[:], rhs=Et[ei][:, 0:128], start=True, stop=False),
                                 reads=[B_const, B_E[ei]], writes=[B_psF[ps_n]])
                            S.op("pe", lambda e, ps_n=ps_n, ei=ei: e.matmul(psF[ps_n][:, 128:256], lhsT=onesb[:], rhs=Et[ei][:, 128:256], start=False, stop=True),
                                 reads=[B_const, B_E[ei]], writes=[B_psF[ps_n]])
                            gb = [B_ag[g_] for g_ in grp]
                            pv = psF[ps_n][:, 0:256].rearrange("p (n q) -> p n q", n=2)
                            if br == 0:
                                S.op("act", lambda e, pv=pv, c=cols(b): e.activation(out=accf[:, :, c], in_=pv, func=AF.Copy), reads=[B_psF[ps_n]], writes=gb)
                            else:
                                S.op("dve", lambda e, pv=pv, c=cols(b): e.tensor_tensor(out=accf[:, :, c], in0=pv, in1=accf[:, :, c], op=ALU.add),
                                     reads=[B_psF[ps_n]] + gb, writes=gb)
                S.op("dve", lambda e: e.reciprocal(out=accf[:, 1, :], in_=accf[:, 1, :]), reads=B_ag, writes=B_ag)
                S.op("pool", lambda e: e.tensor_tensor(out=OT, in0=accf[:, 0, :], in1=accf[:, 1, :], op=ALU.mult), reads=B_ag, writes=[B_OT] + B_ag)
                S.op("sp", lambda e, hr=hr: e.dma_start(out=oT_d[hr, :], in_=OT), reads=[B_OT], writes=[B_o_d], dma_sem=sem_o)
            S.barrier()
            for ti in range(NT):
                load_h(ti, src)
                S.op("sp", lambda e, ti=ti: e.dma_start(out=xT2[:], in_=oT_d[:, ti * 512:(ti + 1) * 512].rearrange("(h p) t -> p h t", p=128)),
                     reads=[B_o_d], writes=B_xT2, dma_sem=sem_x2)
                proj_tokmajor(wo_b[o_idx], B_w[f"wo{l}"], 4, xq_of_unit=lambda u: u)
                finish_tile(ti)
            S.barrier()

        load_consts()
        cast_weights(layers[0])
        src = x
        for li, l in enumerate(layers):
            if "mix" in parts:
                if l % 2 == 0:
                    even_mixer(l, src)
                else:
                    attn_mixer(l, src)
                src = y
                S.barrier()
            if li + 1 < len(layers):
                cast_weights(layers[li + 1])
            if "mlp" in parts:
                mlp_layer(l, src)
                src = y
                S.barrier()
        S.emit(nc)
    nc._needed_inputs = needed
    return nc


def make_consts(hf):
    ident = np.eye(128, dtype=np.float32)
    t = np.arange(128)
    tril = (t[None, :] <= t[:, None]).astype(np.float32)
    k = t[:, None]
    q = t[None, :]
    mask = np.concatenate([(k <= q), (k >= q)], axis=1).astype(np.float32)
    half = 64
    inv_freq = (10000.0 ** (-np.arange(half, dtype=np.float32) * 2.0 / 128)).astype(np.float32)
    pos = (hf * SEQ + np.arange(SEQ)).astype(np.float32)
    ang = (pos[:, None] * inv_freq[None, :]).astype(np.float32)
    cos = np.cos(ang).astype(np.float32).reshape(SEQ // 128, 128, 64).transpose(1, 0, 2)
    sin = np.sin(ang).astype(np.float32).reshape(SEQ // 128, 128, 64).transpose(1, 0, 2)
    return {"c_ident": ident, "c_tril": tril, "c_mask": mask,
            "c_cos": np.ascontiguousarray(cos), "c_sin": np.ascontiguousarray(sin),
            "c_flag": np.full((128, 1), float(hf), dtype=np.float32)}


_PROG_CACHE = {}
_PROG_NEEDED = {}


def run_layers(x, params, layers, trace=False, parts=("mix", "mlp")):
    key = (tuple(layers), tuple(parts))
    if key not in _PROG_CACHE:
        _PROG_CACHE[key] = build_program(layers, parts)
        _PROG_NEEDED[key] = set(_PROG_CACHE[key]._needed_inputs)
    nc = _PROG_CACHE[key]
    needed_names = _PROG_NEEDED[key]
    nb = x.shape[0]
    in_maps = []
    for c in range(8):
        b, hf = (c // 2) % nb, c % 2
        m = {"x": np.ascontiguousarray(x[b, hf * SEQ:(hf + 1) * SEQ])}
        m.update(params)
        m.update(make_consts(hf))
        in_maps.append({k: v for k, v in m.items() if k in needed_names})
    res = run_bass_kernel_spmd(nc, in_maps, core_ids=list(range(8)), trace=trace)
    out = np.empty((nb, 2 * SEQ, D), dtype=np.float32)
    for c in range(2 * nb):
        b, hf = c // 2, c % 2
        out[b, hf * SEQ:(hf + 1) * SEQ] = res.results[c]["y"]
    return out, res


def kernel(x, norm_mix_pre, norm_mix_post, norm_mlp_pre, norm_mlp_post,
           w_in_ab, w_spatial, b_spatial, conv_w, w_out_ab, w_qkv, w_o, w_up, w_down):
    params = {
        "norm_mix_pre": norm_mix_pre, "norm_mix_post": norm_mix_post,
        "norm_mlp_pre": norm_mlp_pre, "norm_mlp_post": norm_mlp_post,
        "w_in_ab": w_in_ab, "w_spatial": w_spatial, "b_spatial": b_spatial, "conv_w": conv_w,
        "w_out_ab": w_out_ab, "w_qkv": w_qkv, "w_o": w_o, "w_up": w_up, "w_down": w_down,
    }
    params = {k: np.ascontiguousarray(np.asarray(v, dtype=np.float32)) for k, v in params.items()}
    x = np.asarray(x, dtype=np.float32)
    out, _ = run_layers(x, params, (0, 1, 2, 3))
    return out.astype(np.float32)
```
